# Optimizing a Trainium2 kernel written in Bass

```python
import math
import jax, jax.numpy as jnp
from jax import lax
import numpy as np

D_MODEL = 1024
BATCH = 32
SEQ = 256
DEPTH = 4
DEC_BATCH = 8
DEC_SEQ = 2048
PAST_LEN = 256

GRID_W = 64
MIX_WIDTH = D_MODEL
HG_WIDTH = MIX_WIDTH // 2
HG_HEADS = 4
HG_DK = HG_WIDTH // HG_HEADS
HG_DV = HG_WIDTH // HG_HEADS
ML_WIDTH = MIX_WIDTH - HG_WIDTH
ML_HEADS = 4
ML_DK = ML_WIDTH // ML_HEADS
ML_DV = ML_WIDTH // ML_HEADS
N_DIR = 2
D_FF = -(-8 * D_MODEL // (3 * 256)) * 256
HG_CHUNK = 16
ML_CHUNK = 64
CONV_K = 3
EPS = 1e-6
D_IN_PROJ = 5 * HG_WIDTH + 4 * ML_WIDTH + 4 * ML_HEADS
SPLIT_POINTS = (HG_WIDTH, 2 * HG_WIDTH, 3 * HG_WIDTH, 4 * HG_WIDTH, 5 * HG_WIDTH,
                5 * HG_WIDTH + 2 * ML_WIDTH, 5 * HG_WIDTH + 3 * ML_WIDTH, 5 * HG_WIDTH + 4 * ML_WIDTH)

kernel_name = 'hymba_hgrn2_mlstm_flow_step'


def rmsnorm(x, w):
    x32 = x.astype(jnp.float32)
    y = x32 * lax.rsqrt(jnp.mean(x32 * x32, axis=-1, keepdims=True) + EPS)
    return (y * w).astype(x.dtype)


def head_rmsnorm(o, n_heads, w):
    b, l, d = o.shape
    o = o.reshape(b, l, n_heads, d // n_heads)
    o = o * lax.rsqrt(jnp.mean(o * o, axis=-1, keepdims=True) + EPS)
    return o.reshape(b, l, d) * w


def split_heads(t, n_heads):
    b, l, d = t.shape
    return t.reshape(b, l, n_heads, d // n_heads).transpose(0, 2, 1, 3)


def merge_heads(t):
    b, h, l, d = t.shape
    return t.transpose(0, 2, 1, 3).reshape(b, l, h * d)


def dwconv2d(x, taps, bias):
    ch = x.shape[-1]
    y = lax.conv_general_dilated(x, taps[:, :, None, :].astype(x.dtype), (1, 1), 'SAME',
                                 dimension_numbers=('NHWC', 'HWIO', 'NHWC'), feature_group_count=ch)
    return y + bias


def gla_chunk_scan(q, k, v, logf, s0):
    bsz, nh, seqlen, dk = q.shape
    dv = v.shape[-1]
    cs = HG_CHUNK
    nc = seqlen // cs
    qc = q.reshape(bsz, nh, nc, cs, dk)
    kc = k.reshape(bsz, nh, nc, cs, dk)
    vc = v.reshape(bsz, nh, nc, cs, dv)
    bcum = jnp.cumsum(logf.reshape(bsz, nh, nc, cs, dk), axis=3)
    causal = jnp.tril(jnp.ones((cs, cs), dtype=bool))[:, :, None]
    rel = bcum[:, :, :, :, None, :] - bcum[:, :, :, None, :, :]
    decay = jnp.exp(jnp.where(causal, rel, -jnp.inf))
    scores = jnp.einsum('bhntd,bhntsd,bhnsd->bhnts', qc, decay, kc)
    o_intra = jnp.einsum('bhnts,bhnsv->bhntv', scores, vc)
    b_last = bcum[:, :, :, -1, :]
    kv_chunk = jnp.einsum('bhnsd,bhnsv->bhndv', kc * jnp.exp(b_last[:, :, :, None, :] - bcum), vc)

    def step(s, inp):
        a_n, kv_n = inp
        return a_n[..., None] * s + kv_n, s

    s_final, s_start = lax.scan(step, s0, (jnp.moveaxis(jnp.exp(b_last), 2, 0),
                                           jnp.moveaxis(kv_chunk, 2, 0)))
    s_start = jnp.moveaxis(s_start, 0, 2)
    o_inter = jnp.einsum('bhntd,bhndv->bhntv', qc * jnp.exp(bcum), s_start)
    return (o_intra + o_inter).reshape(bsz, nh, seqlen, dv), s_final


def mlstm_chunk_scan(q, k, v, log_i, log_f, c0, n0, m0):
    bsz, nh, seqlen, dk = q.shape
    dv = v.shape[-1]
    cs = ML_CHUNK
    nc = seqlen // cs
    qc = q.reshape(bsz, nh, nc, cs, dk)
    kc = k.reshape(bsz, nh, nc, cs, dk)
    vc = v.reshape(bsz, nh, nc, cs, dv)
    ic = log_i.reshape(bsz, nh, nc, cs)
    bcum = jnp.cumsum(log_f.reshape(bsz, nh, nc, cs), axis=3)
    b_last = bcum[..., -1]
    w_end = b_last[..., None] - bcum + ic
    m_loc = jnp.max(w_end, axis=-1)
    p_end = jnp.exp(w_end - m_loc[..., None])
    kv_loc = jnp.einsum('bhns,bhnsd,bhnsv->bhndv', p_end, kc, vc)
    kn_loc = jnp.einsum('bhns,bhnsd->bhnd', p_end, kc)

    def step(carry, inp):
        c_s, n_s, m_s = carry
        bl, ml, kvl, knl = inp
        m_new = jnp.maximum(bl + m_s, ml)
        a = jnp.exp(bl + m_s - m_new)
        g = jnp.exp(ml - m_new)
        c_new = a[..., None, None] * c_s + g[..., None, None] * kvl
        n_new = a[..., None] * n_s + g[..., None] * knl
        return (c_new, n_new, m_new), (c_s, n_s, m_s)

    mv = lambda t: jnp.moveaxis(t, 2, 0)
    (c_f, n_f, m_f), (c_st, n_st, m_st) = lax.scan(
        step, (c0, n0, m0), (mv(b_last), mv(m_loc), mv(kv_loc), mv(kn_loc)))
    c_st = jnp.moveaxis(c_st, 0, 2)
    n_st = jnp.moveaxis(n_st, 0, 2)
    m_st = jnp.moveaxis(m_st, 0, 2)
    causal = jnp.tril(jnp.ones((cs, cs), dtype=bool))
    d_log = jnp.where(causal, bcum[..., :, None] - bcum[..., None, :] + ic[..., None, :], -jnp.inf)
    inter_log = bcum + m_st[..., None]
    m_t = jnp.maximum(inter_log, jnp.max(d_log, axis=-1))
    scores = jnp.einsum('bhntd,bhnsd->bhnts', qc, kc) * jnp.exp(d_log - m_t[..., None])
    a_inter = jnp.exp(inter_log - m_t)
    num = (jnp.einsum('bhnts,bhnsv->bhntv', scores, vc)
           + a_inter[..., None] * jnp.einsum('bhntd,bhndv->bhntv', qc, c_st))
    den = jnp.sum(scores, axis=-1) + a_inter * jnp.einsum('bhntd,bhnd->bhnt', qc, n_st)
    h = num / jnp.maximum(jnp.abs(den), jnp.exp(-m_t))[..., None]
    return h.reshape(bsz, nh, seqlen, dv), c_f, n_f, m_f


def run_direction(scan_fn, reverse, seq_args, state_args):
    if reverse:
        seq_args = [jnp.flip(a, axis=2) for a in seq_args]
    o, *fin = scan_fn(*seq_args, *state_args)
    if reverse:
        o = jnp.flip(o, axis=2)
    return o, fin


def token_mix(h, grid_hw, conv_taps, conv_b_l, states, lb_l, w_in_l, ml_gate_b_l,
              hg_norm_w_l, ml_norm_w_l, w_out_l):
    bsz, seqlen, _ = h.shape
    proj = jnp.einsum('bld,de->ble', h, w_in_l).astype(jnp.float32)
    hg_q, hg_ff, hg_fb, hg_i, hg_g, ml_qk, ml_v, ml_o, ml_gates = jnp.split(proj, SPLIT_POINTS, axis=-1)
    hg_s0, ml_c0, ml_n0, ml_m0 = [s.astype(jnp.float32) for s in states]

    q_h = split_heads(jax.nn.silu(hg_q), HG_HEADS)
    v_h = split_heads(hg_i, HG_HEADS)
    hg_outs, hg_fin = [], []
    for d, fz in enumerate((hg_ff, hg_fb)):
        lb = lb_l[d].astype(jnp.float32)
        logf = jnp.logaddexp(jnp.log(lb), jnp.log1p(-lb) + jax.nn.log_sigmoid(fz))
        k_h = (1.0 - lb) * jax.nn.sigmoid(-fz)
        o, fin = run_direction(gla_chunk_scan, d == 1,
                               (q_h, split_heads(k_h, HG_HEADS), v_h, split_heads(logf, HG_HEADS)),
                               (hg_s0[:, d],))
        hg_outs.append(o)
        hg_fin.append(fin[0])
    hg_out = head_rmsnorm(merge_heads(hg_outs[0] + hg_outs[1]), HG_HEADS, hg_norm_w_l) * jax.nn.silu(hg_g)

    rows, cols = grid_hw
    qk = dwconv2d(ml_qk.reshape(bsz, rows, cols, 2 * ML_WIDTH), conv_taps, conv_b_l)
    qk = jax.nn.silu(qk.reshape(bsz, seqlen, 2 * ML_WIDTH))
    mq, mk = jnp.split(qk, 2, axis=-1)
    q_m = split_heads(mq, ML_HEADS)
    k_m = split_heads(mk, ML_HEADS) * (ML_DK ** -0.5)
    v_m = split_heads(ml_v, ML_HEADS)
    gates = (ml_gates + ml_gate_b_l).reshape(bsz, seqlen, 4, ML_HEADS).transpose(0, 2, 3, 1)
    ml_outs, c_fin, n_fin, m_fin = [], [], [], []
    for d in range(N_DIR):
        o, fin = run_direction(mlstm_chunk_scan, d == 1,
                               (q_m, k_m, v_m, gates[:, d], jax.nn.log_sigmoid(gates[:, 2 + d])),
                               (ml_c0[:, d], ml_n0[:, d], ml_m0[:, d]))
        ml_outs.append(o)
        c_fin.append(fin[0])
        n_fin.append(fin[1])
        m_fin.append(fin[2])
    ml_out = head_rmsnorm(merge_heads(ml_outs[0] + ml_outs[1]), ML_HEADS, ml_norm_w_l) * jax.nn.sigmoid(ml_o)

    mix = jnp.concatenate([hg_out, ml_out], axis=-1).astype(h.dtype)
    out = jnp.einsum('ble,ed->bld', mix, w_out_l)
    new_states = (jnp.stack(hg_fin, axis=1), jnp.stack(c_fin, axis=1),
                  jnp.stack(n_fin, axis=1), jnp.stack(m_fin, axis=1))
    return out, new_states


def trunk_layer(x, cond, grid_hw, conv_taps, states, n1, n2, w_mod_l, b_mod_l, conv_b_l, lb_l,
                w_in_l, ml_gate_b_l, hg_norm_w_l, ml_norm_w_l, w_out_l, w_gate_l, w_up_l, w_down_l):
    mod = (jax.nn.silu(cond) @ w_mod_l + b_mod_l)[:, None, :]
    sh1, sc1, g1, sh2, sc2, g2 = jnp.split(mod, 6, axis=-1)
    h = rmsnorm(x, n1) * (1.0 + sc1) + sh1
    mix, new_states = token_mix(h, grid_hw, conv_taps, conv_b_l, states, lb_l, w_in_l, ml_gate_b_l,
                                hg_norm_w_l, ml_norm_w_l, w_out_l)
    x = x + g1 * mix
    h = rmsnorm(x, n2) * (1.0 + sc2) + sh2
    ffn = (jax.nn.silu(h @ w_gate_l) * (h @ w_up_l)) @ w_down_l
    x = x + g2 * ffn
    return x, new_states


def setup_inputs(seed: int = 0) -> dict:
    key = jax.random.key(seed)
    ks = jax.random.split(key, 32)
    nrm = lambda k, shape, s=1.0: s * jax.random.normal(k, shape, jnp.float32)
    x_prompt = nrm(ks[0], (BATCH, SEQ, D_MODEL))
    x_sample = nrm(ks[1], (DEC_BATCH, DEC_SEQ, D_MODEL))
    state_hgrn = nrm(ks[2], (DEC_BATCH, DEPTH, N_DIR, HG_HEADS, HG_DK, HG_DV), 0.5)
    state_mlstm_c = nrm(ks[3], (DEC_BATCH, DEPTH, N_DIR, ML_HEADS, ML_DK, ML_DV), 0.5)
    state_mlstm_n = nrm(ks[4], (DEC_BATCH, DEPTH, N_DIR, ML_HEADS, ML_DK), 0.5)
    state_mlstm_m = nrm(ks[5], (DEC_BATCH, DEPTH, N_DIR, ML_HEADS), 0.5)
    c = nrm(ks[6], (DEC_BATCH, D_MODEL))
    c_ctx = nrm(ks[7], (D_MODEL,))
    norm1_w = 1.0 + nrm(ks[8], (DEPTH, D_MODEL), 0.02)
    norm2_w = 1.0 + nrm(ks[9], (DEPTH, D_MODEL), 0.02)
    w_mod = nrm(ks[10], (DEPTH, D_MODEL, 6 * D_MODEL), 0.5 * D_MODEL ** -0.5)
    b_mod = nrm(ks[11], (DEPTH, 6 * D_MODEL), 0.02)
    w_in = nrm(ks[12], (DEPTH, D_MODEL, D_IN_PROJ), D_MODEL ** -0.5)
    conv_w = nrm(ks[13], (DEPTH, CONV_K, CONV_K, 2 * ML_WIDTH), 1.0 / CONV_K)
    conv_b = nrm(ks[14], (DEPTH, 2 * ML_WIDTH), 0.02)
    ml_gate_b = jnp.concatenate([nrm(ks[15], (DEPTH, 2 * ML_HEADS), 0.1),
                                 3.0 + 3.0 * jax.random.uniform(ks[16], (DEPTH, 2 * ML_HEADS), jnp.float32)],
                                axis=-1)
    hg_lb_logits = 1.0 + nrm(ks[17], (DEPTH, N_DIR, HG_WIDTH), 0.5)
    hg_norm_w = 1.0 + nrm(ks[18], (DEPTH, HG_WIDTH), 0.02)
    ml_norm_w = 1.0 + nrm(ks[19], (DEPTH, ML_WIDTH), 0.02)
    w_out = nrm(ks[20], (DEPTH, MIX_WIDTH, D_MODEL), MIX_WIDTH ** -0.5)
    w_gate = nrm(ks[21], (DEPTH, D_MODEL, D_FF), D_MODEL ** -0.5)
    w_up = nrm(ks[22], (DEPTH, D_MODEL, D_FF), D_MODEL ** -0.5)
    w_down = nrm(ks[23], (DEPTH, D_FF, D_MODEL), D_FF ** -0.5)
    final_norm_w = 1.0 + nrm(ks[24], (D_MODEL,), 0.02)
    return {'x_prompt': x_prompt, 'x_sample': x_sample, 'state_hgrn': state_hgrn,
            'state_mlstm_c': state_mlstm_c, 'state_mlstm_n': state_mlstm_n, 'state_mlstm_m': state_mlstm_m,
            'c': c, 'c_ctx': c_ctx, 'norm1_w': norm1_w, 'norm2_w': norm2_w, 'w_mod': w_mod, 'b_mod': b_mod,
            'w_in': w_in, 'conv_w': conv_w, 'conv_b': conv_b, 'ml_gate_b': ml_gate_b,
            'hg_lb_logits': hg_lb_logits, 'hg_norm_w': hg_norm_w, 'ml_norm_w': ml_norm_w, 'w_out': w_out,
            'w_gate': w_gate, 'w_up': w_up, 'w_down': w_down, 'final_norm_w': final_norm_w}


def reference(x_prompt, x_sample, state_hgrn, state_mlstm_c, state_mlstm_n, state_mlstm_m, c, c_ctx,
              norm1_w, norm2_w, w_mod, b_mod, w_in, conv_w, conv_b, ml_gate_b, hg_lb_logits,
              hg_norm_w, ml_norm_w, w_out, w_gate, w_up, w_down, final_norm_w):
    lb_all = jnp.cumsum(jax.nn.softmax(hg_lb_logits.astype(jnp.float32), axis=0), axis=0)
    lb_all = lb_all - lb_all[0]
    n_ctx_req = x_prompt.shape[0]
    ctx_grid = (1, x_prompt.shape[1])
    rows = x_sample.shape[1] // GRID_W
    lat_grid = (rows, GRID_W)
    zero_states = (jnp.zeros((n_ctx_req, N_DIR, HG_HEADS, HG_DK, HG_DV), jnp.float32),
                   jnp.zeros((n_ctx_req, N_DIR, ML_HEADS, ML_DK, ML_DV), jnp.float32),
                   jnp.zeros((n_ctx_req, N_DIR, ML_HEADS, ML_DK), jnp.float32),
                   jnp.zeros((n_ctx_req, N_DIR, ML_HEADS), jnp.float32))
    xp, xs = x_prompt, x_sample
    hg_st, mc_st, mn_st, mm_st = [], [], [], []
    for l in range(DEPTH):
        shared = (norm1_w[l], norm2_w[l], w_mod[l], b_mod[l], conv_b[l], lb_all[l], w_in[l], ml_gate_b[l],
                  hg_norm_w[l], ml_norm_w[l], w_out[l], w_gate[l], w_up[l], w_down[l])
        xp, st = trunk_layer(xp, c_ctx[None, :], ctx_grid, conv_w[l, 1:2], zero_states, *shared)
        hg_st.append(st[0])
        mc_st.append(st[1])
        mn_st.append(st[2])
        mm_st.append(st[3])
        cached = (state_hgrn[:, l], state_mlstm_c[:, l], state_mlstm_n[:, l], state_mlstm_m[:, l])
        xs, _ = trunk_layer(xs, c, lat_grid, conv_w[l], cached, *shared)
    y_prompt = rmsnorm(xp, final_norm_w)
    y_sample = rmsnorm(xs, final_norm_w)
    return (y_prompt, y_sample, jnp.stack(hg_st, axis=1), jnp.stack(mc_st, axis=1),
            jnp.stack(mn_st, axis=1), jnp.stack(mm_st, axis=1))
```

```python
import numpy as np
from contextlib import ExitStack
import concourse.bass as bass
import concourse.mybir as mybir
from concourse.bass_utils import run_bass_kernel_spmd

F32 = mybir.dt.float32
BF16 = mybir.dt.bfloat16
ALU = mybir.AluOpType
AF = mybir.ActivationFunctionType
AX = mybir.AxisListType

ENGS = ["pe", "act", "dve", "pool", "sp"]
DEPTH = 4
D = 1024
DFF = 2816
NFF = 22
DIN = 4624
EPS = 1e-6


class Buf:
    __slots__ = ("name", "last_w", "readers")

    def __init__(self, name):
        self.name = name
        self.last_w = None
        self.readers = []


class Sched:
    def __init__(self, nc):
        self.nc = nc
        self.ops = {e: [] for e in ENGS}
        self.seen = {e: {} for e in ENGS}
        self.dma_cnt = {}
        self.sig = set()
        self.final_waits = []

    def _deps(self, eng, reads, writes):
        deps = []
        for b in reads:
            if b.last_w is not None:
                deps.append(b.last_w)
        for b in writes:
            if b.last_w is not None:
                deps.append(b.last_w)
            deps.extend(b.readers)
        seen = self.seen[eng]
        best = {}
        for ev in deps:
            if ev[0] == "c" and ev[1] == "pe" and eng == "pe":
                continue
            key = (ev[0], ev[1])
            if seen.get(key, -1) >= ev[2]:
                continue
            if key not in best or best[key][2] < ev[2]:
                best[key] = ev
        for key, ev in best.items():
            seen[key] = ev[2]
            if ev[0] == "c":
                self.sig.add((ev[1], ev[2]))
        return list(best.values())

    def op(self, eng, fn, reads=(), writes=()):
        waits = self._deps(eng, reads, writes)
        idx = len(self.ops[eng])
        ev = ("c", eng, idx)
        self.ops[eng].append((waits, fn, None))
        for b in reads:
            b.readers.append(ev)
        for b in writes:
            b.last_w = ev
            b.readers = []
        return ev

    def dma(self, eng, fn, key, reads=(), writes=(), n=1):
        waits = self._deps(eng, reads, writes)
        val = self.dma_cnt.get(key, 0) + 16 * n
        self.dma_cnt[key] = val
        ev = ("d", key, val)
        self.ops[eng].append((waits, fn, key))
        for b in reads:
            b.readers.append(ev)
        for b in writes:
            b.last_w = ev
            b.readers = []
        return ev

    def finish(self, eng, bufs):
        evs = []
        for b in bufs:
            if b.last_w is not None:
                evs.append(b.last_w)
            evs.extend(b.readers)
        self.final_waits.append((eng, evs))
        for ev in evs:
            if ev[0] == "c":
                self.sig.add((ev[1], ev[2]))

    def emit(self, stack):
        nc = self.nc
        sems = {e: stack.enter_context(nc.semaphore("s_" + e)) for e in ENGS}
        dsems = {k: stack.enter_context(nc.semaphore("d_" + str(k))) for k in self.dma_cnt}
        rank = {}
        for e in ENGS:
            r = 0
            for i, o in enumerate(self.ops[e]):
                if o[2] is None and (e, i) in self.sig:
                    r += 1
                    rank[(e, i)] = r

        def run(e, engine):
            def w(ev):
                if ev[0] == "c":
                    engine.wait_ge(sems[ev[1]], rank[(ev[1], ev[2])])
                else:
                    engine.wait_ge(dsems[ev[1]], ev[2])
            for i, (waits, fn, dkey) in enumerate(self.ops[e]):
                for ev in waits:
                    w(ev)
                if dkey is not None:
                    res = fn(engine)
                    if not isinstance(res, (list, tuple)):
                        res = [res]
                    for ins in res:
                        ins.then_inc(dsems[dkey], 16)
                else:
                    ins = fn(engine)
                    if (e, i) in rank:
                        ins.then_inc(sems[e], 1)
            for (fe, evs) in self.final_waits:
                if fe == e:
                    best = {}
                    for ev in evs:
                        k = (ev[0], ev[1])
                        if k not in best or best[k][2] < ev[2]:
                            best[k] = ev
                    for ev in best.values():
                        w(ev)

        block = stack.enter_context(nc.Block())

        @block.tensor
        def _(eng):
            run("pe", eng)

        @block.scalar
        def _(eng):
            run("act", eng)

        @block.vector
        def _(eng):
            run("dve", eng)

        @block.gpsimd
        def _(eng):
            run("pool", eng)

        @block.sync
        def _(eng):
            run("sp", eng)


def V3(ap, dims):
    return bass.AP(ap.tensor, ap.offset, [list(ap.ap[0])] + [list(d) for d in dims])


C_IDENT = 0
C_MASKU = 128
C_MASKL = 256
C_RMF = 384
C_RMB = 896
C_NEGBIG = 1408
C_ONES = 1536
C_ZERO = 1664
C_KC = 2176
NCONST = 2180


def host_consts():
    c = np.zeros((128, NCONST), np.float32)
    c[:, C_IDENT:C_IDENT + 128] = np.eye(128, dtype=np.float32)
    s = np.arange(128)[:, None]
    t = np.arange(128)[None, :]
    c[:, C_MASKU:C_MASKU + 128] = (s <= t).astype(np.float32)
    c[:, C_MASKL:C_MASKL + 128] = (s >= t).astype(np.float32)
    tt = np.arange(512)
    c[:, C_RMF:C_RMF + 512] = (tt % 128 != 0).astype(np.float32)[None, :]
    c[:, C_RMB:C_RMB + 512] = (tt % 128 != 127).astype(np.float32)[None, :]
    c[:, C_NEGBIG:C_NEGBIG + 128] = -1e30
    c[:, C_ONES:C_ONES + 128] = 1.0
    c[:, C_KC + 1] = 1.0
    c[:, C_KC + 2] = EPS
    c[:, C_KC + 3] = 75.0
    sel = np.zeros((16, 16 * 128), np.float32)
    for r in range(16):
        sel[r, r * 128:(r + 1) * 128] = 1.0
    return c, sel


class Job:
    def __init__(self, name, L, seqlen, cond, conv2d, init_cache, state_out):
        self.name = name
        self.L = L
        self.seqlen = seqlen
        self.nseq = L // seqlen
        self.G = L // 512
        self.NT = L // 128
        self.tps = seqlen // 128
        self.cond = cond
        self.conv2d = conv2d
        self.init_cache = init_cache
        self.state_out = state_out


def build_program(nlayers=DEPTH, do_hg=True, do_ml=True, do_ffn=True):
    nc = bass.Bass("TRN2", target_bir_lowering=False)
    dt_in = lambda name, shape: nc.dram_tensor(name, list(shape), F32, kind="ExternalInput").ap()
    dt_out = lambda name, shape: nc.dram_tensor(name, list(shape), F32, kind="ExternalOutput").ap()

    x_s = dt_in("x_s", (2048, D))
    x_p = dt_in("x_p", (1024, D))
    cvec = dt_in("cvec", (16, 128))
    st_hg = dt_in("st_hg", (4, 2, 4, 128, 128))
    st_c = dt_in("st_c", (4, 2, 4, 128, 128))
    st_n = dt_in("st_n", (32, 128))
    st_m = dt_in("st_m", (1, 32))
    norm1_w = dt_in("norm1_w", (4, D))
    norm2_w = dt_in("norm2_w", (4, D))
    w_mod = dt_in("w_mod", (4, D, 6 * D))
    b_mod = dt_in("b_mod", (4, 6 * D))
    w_in = dt_in("w_in", (4, D, DIN))
    conv_w = dt_in("conv_w", (4, 3, 3, D))
    conv_b = dt_in("conv_b", (4, D))
    ml_gate_b = dt_in("ml_gate_b", (4, 16))
    hg_lb_logits = dt_in("hg_lb_logits", (4, 2, 512))
    hg_norm_w = dt_in("hg_norm_w", (4, 512))
    ml_norm_w = dt_in("ml_norm_w", (4, 512))
    w_out = dt_in("w_out", (4, D, D))
    w_gate = dt_in("w_gate", (4, D, DFF))
    w_up = dt_in("w_up", (4, D, DFF))
    w_down = dt_in("w_down", (4, DFF, D))
    final_norm_w = dt_in("final_norm_w", (D,))
    consts = dt_in("consts", (128, NCONST))
    selc = dt_in("selc", (16, 2048))

    y_s = dt_out("y_s", (2048, D))
    y_p = dt_out("y_p", (1024, D))
    ns_hg = dt_out("ns_hg", (4, 4, 2, 4, 128, 128))
    ns_c = dt_out("ns_c", (4, 4, 2, 4, 128, 128))
    ns_n = dt_out("ns_n", (4, 4, 2, 4, 128))
    ns_m = dt_out("ns_m", (4, 4, 2, 4))

    xd_s = nc.dram_tensor("xd_s", [128, 8, 2048], F32).ap()
    xd_p = nc.dram_tensor("xd_p", [128, 8, 1024], F32).ap()

    jobs = [Job("S", 2048, 2048, 0, True, True, False),
            Job("P", 1024, 256, 1, False, False, True)]
    xin = {"S": x_s, "P": x_p}
    xd = {"S": xd_s, "P": xd_p}
    yout = {"S": y_s, "P": y_p}

    S = Sched(nc)
    st = ExitStack()

    bufs = {}

    def sb(name, shape, dt=F32):
        t = st.enter_context(nc.sbuf_tensor(name, list(shape), dt))
        bufs[name] = Buf(name)
        return t

    def OP(eng, method, reads, writes, *args, **kw):
        return S.op(eng, lambda e: getattr(e, method)(*args, **kw), reads, writes)

    def DMA(eng, key, reads, writes, out, in_):
        return S.dma(eng, lambda e: e.dma_start(out=out, in_=in_), key, reads, writes)

    def sigm(src, bsrc, ta, bta, tb, btb):
        OP("act", "activation", bsrc, [bta], out=ta, in_=src, func=AF.Exp, scale=-1.0)
        OP("act", "activation", [bta, bCSTh[0]], [btb], out=tb, in_=ta, func=AF.Ln, bias=ONEh[0], scale=1.0)
        OP("act", "activation", [btb], [bta], out=ta, in_=tb, func=AF.Exp, scale=-1.0)

    bCSTh = [None]
    ONEh = [None]
    PS = []
    PSB = []
    for i in range(7):
        PS.append(st.enter_context(nc.psum_tensor("ps%d" % i, [128, 512], F32)))
        PSB.append(Buf("ps%d" % i))
    PKT = st.enter_context(nc.psum_tensor("pkt", [128, 512], BF16))
    PKTB = Buf("pkt")
    PA, PB_, PAT, PKV, PO, PDEN, PMISC = PS
    bPA, bPB, bPAT, bPKV, bPO, bPDEN, bPMISC = PSB

    CST = sb("CST", [128, NCONST])
    DMA("sp", "ld_cst", [], [bufs["CST"]], CST[:], consts[:, :])
    ident = CST[:, C_IDENT:C_IDENT + 128]
    CB = sb("CB", [128, 3 * 128], BF16)
    OP("dve", "tensor_copy", [bufs["CST"]], [bufs["CB"]], out=CB[:], in_=CST[:, 0:384])
    ident_bf = CB[:, 0:128]
    ONES = sb("ONES", [128, 128], BF16)
    OP("dve", "tensor_copy", [bufs["CST"]], [bufs["ONES"]], out=ONES[:], in_=CST[:, C_ONES:C_ONES + 128])
    KC = CST[:, C_KC:C_KC + 4]
    ZERO = KC[:, 0:1]
    ONE = KC[:, 1:2]
    EPSC = KC[:, 2:3]
    C60 = KC[:, 3:4]
    bCST = bufs["CST"]
    bKC = bCST
    bCB = bufs["CB"]
    bONES = bufs["ONES"]
    ZR = CST[:, C_ZERO:C_ZERO + 512]
    bCSTh[0] = bCST
    ONEh[0] = ONE
    for i in range(7):
        OP("dve", "tensor_copy", [bCST], [PSB[i]], out=PS[i][:], in_=ZR)

    vec_src = [
        ("n1", norm1_w.rearrange("l (c p) -> (l c) p", p=128)),
        ("n2", norm2_w.rearrange("l (c p) -> (l c) p", p=128)),
        ("fn", final_norm_w.rearrange("(c p) -> c p", p=128)),
        ("bm", b_mod.rearrange("l (c p) -> (l c) p", p=128)),
        ("cw", conv_w.rearrange("l a b (c p) -> (l a b c) p", p=128)),
        ("cb", conv_b.rearrange("l (c p) -> (l c) p", p=128)),
        ("lb", hg_lb_logits.rearrange("l d (h p) -> (l d h) p", p=128)),
        ("hn", hg_norm_w.rearrange("l (h p) -> (l h) p", p=128)),
        ("mn", ml_norm_w.rearrange("l (h p) -> (l h) p", p=128)),
        ("cv", cvec),
        ("n0", st_n),
    ]
    voff = {}
    r = 0
    for name, ap in vec_src:
        voff[name] = r
        r += ap.shape[0]
    RT = r
    NCH = (RT + 127) // 128
    VST = sb("VST", [128, NCH, 128])
    V = sb("V", [128, NCH * 128])
    bVST = bufs["VST"]
    bV = bufs["V"]
    for ch_ in range(NCH):
        OP("pool", "tensor_copy", [bCST], [bVST], out=VST[:, ch_, :], in_=ZR[:, 0:128])
    for name, ap in vec_src:
        r0 = voff[name]
        n = ap.shape[0]
        done = 0
        while done < n:
            row = r0 + done
            ch, pr = row // 128, row % 128
            cnt = min(n - done, 128 - pr)
            DMA("sp", "ld_vst", [], [bVST], VST[pr:pr + cnt, ch, :], ap[done:done + cnt, :])
            done += cnt
    for ch in range(NCH):
        OP("pe", "transpose", [bVST, bCST], [bPA], PA[:, 0:128], VST[:, ch, :], ident)
        OP("act", "copy", [bPA], [bV], out=V[:, ch * 128:(ch + 1) * 128], in_=PA[:, 0:128])

    def vcol(name, idx):
        c = voff[name] + idx
        return V[:, c:c + 1]

    MB = sb("MB", [128, 32])
    DMA("sp", "ld_mb", [], [bufs["MB"]], MB[:], bass.AP(st_m.tensor, st_m.offset, [[0, 128], [1, 32]]))
    GB = sb("GB", [16, 4])
    S.dma("sp", lambda e: e.dma_start(out=GB[:], in_=ml_gate_b.rearrange("l g -> g l"), allow_slow_non_contiguous=True), "ld_gb", [], [bufs["GB"]])

    LBE = sb("LBE", [128, 32])
    LB = sb("LB", [128, 32])
    OML = sb("OML", [128, 32])
    LBT = sb("LBT", [128, 16])
    bLB = bufs["LB"]
    lo = voff["lb"]
    OP("act", "activation", [bV], [bufs["LBE"]], out=LBE[:], in_=V[:, lo:lo + 32], func=AF.Exp)
    OP("dve", "tensor_tensor", [bufs["LBE"]], [bufs["LBT"]], out=LBT[:, 0:8], in0=LBE[:, 0:8], in1=LBE[:, 8:16], op=ALU.add)
    OP("dve", "tensor_tensor", [bufs["LBE"], bufs["LBT"]], [bufs["LBT"]], out=LBT[:, 0:8], in0=LBT[:, 0:8], in1=LBE[:, 16:24], op=ALU.add)
    OP("dve", "tensor_tensor", [bufs["LBE"], bufs["LBT"]], [bufs["LBT"]], out=LBT[:, 0:8], in0=LBT[:, 0:8], in1=LBE[:, 24:32], op=ALU.add)
    OP("dve", "reciprocal", [bufs["LBT"]], [bufs["LBT"]], out=LBT[:, 8:16], in_=LBT[:, 0:8])
    OP("dve", "tensor_copy", [bCST], [bLB], out=LB[:, 0:8], in_=ZR[:, 0:8])
    OP("dve", "tensor_tensor", [bufs["LBE"], bufs["LBT"]], [bLB], out=LB[:, 8:16], in0=LBE[:, 8:16], in1=LBT[:, 8:16], op=ALU.mult)
    for l in (2, 3):
        OP("dve", "tensor_tensor", [bufs["LBE"], bufs["LBT"]], [bufs["LBE"]], out=LBE[:, 0:8], in0=LBE[:, l * 8:l * 8 + 8], in1=LBT[:, 8:16], op=ALU.mult)
        OP("dve", "tensor_tensor", [bufs["LBE"], bLB], [bLB], out=LB[:, l * 8:l * 8 + 8], in0=LBE[:, 0:8], in1=LB[:, (l - 1) * 8:l * 8], op=ALU.add)
    OP("dve", "tensor_scalar", [bLB], [bufs["OML"]], out=OML[:], in0=LB[:], scalar1=-1.0, scalar2=1.0, op0=ALU.mult, op1=ALU.add)
    bOML = bufs["OML"]

    NSLOT = 3
    WS = [sb("WS%d" % i, [128, 4096], BF16) for i in range(NSLOT)]
    WSB = [bufs["WS%d" % i] for i in range(NSLOT)]

    def rows(w2d):
        return w2d.rearrange("(c p) n -> p c n", p=128)

    witems = []

    def item_blk(w2d, c0, ncol, kch=8):
        def mk(slot):
            v = slot[:, 0:kch * ncol].rearrange("p (c n) -> p c n", c=kch)
            return [(v, rows(w2d)[:, :, c0:c0 + ncol])]
        return mk

    def item_heads(w2d, cols):
        def mk(slot):
            v = slot[:, 0:8 * len(cols) * 128].rearrange("p (c b n) -> p c b n", c=8, b=len(cols))
            return [(v[:, :, i, :], rows(w2d)[:, :, c0:c0 + 128]) for i, c0 in enumerate(cols)]
        return mk

    def item_wd(w2d, dc):
        def mk(slot):
            v = slot[:, 0:NFF * 128].rearrange("p (f n) -> p f n", f=NFF)
            return [(v, w2d.rearrange("(f p) n -> p f n", p=128)[:, :, dc * 128:(dc + 1) * 128])]
        return mk

    for l in range(nlayers):
        for pc in range(12):
            witems.append(item_blk(w_mod[l], pc * 512, 512))
    for job in jobs:
        for l in range(nlayers):
            if do_hg:
                witems.append(item_blk(w_in[l], 3 * 512, 512))
                for h in range(4):
                    witems.append(item_heads(w_in[l], [0 * 512 + h * 128, 1 * 512 + h * 128, 2 * 512 + h * 128, 4 * 512 + h * 128]))
            if do_ml:
                witems.append(item_blk(w_in[l], 7 * 512, 512))
                for h in range(4):
                    witems.append(item_heads(w_in[l], [5 * 512 + h * 128, 6 * 512 + h * 128, 8 * 512 + h * 128]))
            if do_hg or do_ml:
                witems.append(item_blk(w_out[l], 0, 512))
                witems.append(item_blk(w_out[l], 512, 512))
            if do_ffn:
                for tg in range(job.L // 1024):
                    for b in range(6):
                        ncol = 512 if b < 5 else 256
                        witems.append(item_blk(w_gate[l], b * 512, ncol))
                        witems.append(item_blk(w_up[l], b * 512, ncol))
                    for dc in range(8):
                        witems.append(item_wd(w_down[l], dc))
    wstate = {"cur": 0, "issued": 0}

    def w_issue(k):
        slot = k % NSLOT
        pairs = witems[k](WS[slot])

        def fn(e, pairs=pairs):
            return [e.dma_start(out=o, in_=i) for (o, i) in pairs]
        S.dma("pool", fn, "w%d" % slot, [], [WSB[slot]], n=len(pairs))

    def w_take():
        k = wstate["cur"]
        while wstate["issued"] < min(len(witems), k + NSLOT - 1):
            w_issue(wstate["issued"])
            wstate["issued"] += 1
        if wstate["issued"] <= k:
            w_issue(k)
            wstate["issued"] = k + 1
        wstate["cur"] = k + 1
        return WS[k % NSLOT], WSB[k % NSLOT]

    CSF = sb("CSF", [128, 8, 2])
    CSB = sb("CSB", [128, 8, 2], BF16)
    cvo = voff["cv"]
    OP("dve", "tensor_copy", [bV], [bufs["CSF"]], out=CSF[:, :, 0], in_=V[:, cvo:cvo + 8])
    OP("dve", "tensor_copy", [bV], [bufs["CSF"]], out=CSF[:, :, 1], in_=V[:, cvo + 8:cvo + 16])
    OP("act", "activation", [bufs["CSF"]], [bufs["CSB"]], out=CSB[:], in_=CSF[:], func=AF.Silu)
    MODS = sb("MODS", [128, 4, 48, 2])
    bMODS = bufs["MODS"]
    for l in range(nlayers):
        for pc in range(12):
            slot, sbuf_ = w_take()
            wv = slot[:, 0:4096].rearrange("p (c n) -> p c n", c=8)
            for m in range(4):
                blk = pc * 4 + m
                for c in range(8):
                    OP("pe", "matmul", [sbuf_, bufs["CSB"]], [bPMISC], PMISC[:, blk * 2:blk * 2 + 2],
                       lhsT=wv[:, c, m * 128:(m + 1) * 128], rhs=CSB[:, c, :], start=(c == 0), stop=(c == 7))
        bo = voff["bm"] + l * 48
        OP("dve", "tensor_tensor", [bPMISC, bV], [bMODS], out=MODS[:, l, :, :],
           in0=PMISC[:, 0:96].rearrange("p (m j) -> p m j", j=2),
           in1=V3(V[:, bo:bo + 48], [[1, 48], [0, 2]]), op=ALU.add)
    SC = sb("SC", [128, 4, 2, 8, 2])
    bSC = bufs["SC"]
    for l in range(nlayers):
        for wh, (mo, nn) in enumerate(((8, "n1"), (32, "n2"))):
            no = voff[nn] + l * 8
            OP("dve", "scalar_tensor_tensor", [bMODS, bV], [bSC], out=SC[:, l, wh, :, :], in0=MODS[:, l, mo:mo + 8, :],
               scalar=1.0, in1=V3(V[:, no:no + 8], [[1, 8], [0, 2]]), op0=ALU.add, op1=ALU.mult)

    def modc(l, which, c, j):
        return MODS[:, l, which * 8 + c, j:j + 1]

    REGA = sb("REGA", [128, 4096])
    REGB = sb("REGB", [128, 4096])
    REGC = sb("REGC", [128, 2048])
    XG = [REGA[:, :].rearrange("p (c n) -> p c n", c=8), REGB[:, :].rearrange("p (c n) -> p c n", c=8)]
    bXG = [Buf("xg0"), Buf("xg1")]
    XTOK = [REGC[:, 0:1024], REGC[:, 1024:2048]]
    bXTOK = [Buf("xtok0"), Buf("xtok1")]
    SQK = sb("SQK", [128, 4096], BF16)
    SQ = SQK[:, :].rearrange("p (c n) -> p c n", c=8)
    bSQ = Buf("sq")
    RSTD = sb("RSTD", [128, 512])
    bRSTD = bufs["RSTD"]
    LNV = sb("LNV", [128, 512])
    bLNV = bufs["LNV"]
    LMAX = 2048
    HT = sb("HT", [128, 8, LMAX], BF16)
    bHT = [Buf("ht%d" % g) for g in range(LMAX // 512)]
    mixd = {"S": nc.dram_tensor("mixd_s", [128, 8, 2048], BF16).ap(), "P": nc.dram_tensor("mixd_p", [128, 8, 1024], BF16).ap()}
    bMIXD = {"S": [Buf("mixdS%d" % g) for g in range(4)], "P": [Buf("mixdP%d" % g) for g in range(2)]}
    VTOK = REGB[:, :].bitcast(BF16).rearrange("p (t n) -> p t n", n=512)
    bVTOK = [Buf("vtok%d" % g) for g in range(LMAX // 512)]
    ARENA = sb("ARENA", [128, 44 * 1024 // 4])
    def arena_f32(off_words, n):
        return ARENA[:, off_words:off_words + n]
    QF = arena_f32(0, 2048)
    OACC = arena_f32(2048, 2048)
    PRE = arena_f32(0, 2048)
    HACC = PRE
    ACC = arena_f32(2048, 2048)
    QKC = ARENA[:, 4096:6144].bitcast(BF16)
    QC = QKC[:, 0:2048]
    KCV = QKC[:, 2048:4096]
    G16 = ARENA[0:16, 6144:6144 + 2048]
    ABUF = ARENA[:, 0:11264].bitcast(BF16)
    MIXG = ARENA[:, 8192:10240].bitcast(BF16).rearrange("p (c n) -> p c n", c=8)
    bMIXG = Buf("mixg")
    SEL = ARENA[0:16, 8192:10240]
    bSEL = Buf("sel")
    bQF = [Buf("qf%d" % g) for g in range(4)]
    bOACC = [Buf("oacc%d" % g) for g in range(4)]
    bHACC = [Buf("hacc%d" % g) for g in range(4)]
    bPRE = bHACC
    bACC = Buf("acc")
    bQC = [Buf("qc%d" % g) for g in range(4)]
    bKCV = [Buf("kcv%d" % g) for g in range(4)]
    bG16 = [Buf("g16%d" % g) for g in range(4)]
    bABUF = [Buf("abuf%d" % s_) for s_ in range(2)]
    arena_all = bQF + bOACC + [bACC] + bHACC + bQC + bKCV + bG16 + bABUF + [bMIXG, bSEL]

    NT_ = 12
    TMP = [REGC[:, i * 512:(i + 1) * 512] for i in range(4)] + [REGA[:, i * 512:(i + 1) * 512] for i in range(8)]
    bTMP = [Buf("T%d" % i) for i in range(NT_)]
    TB = [sb("TB%d" % i, [128, 512], BF16) for i in range(6)]
    bTB = [bufs["TB%d" % i] for i in range(6)]
    KHAT = SQK[:, :].rearrange("p (i a b) -> p i a b", i=8, a=4)
    bKHAT = Buf("khat")
    FDUM = sb("FDUM", [128, 2])
    def get_alias():
        return bXG + bXTOK + [bSQ, bKHAT, bufs["SQK2"]] + bMLX + bVTOK + arena_all + bTMP

    def interleave(a, b, ratio=(1, 1)):
        da = db = False
        if b is None:
            db = True
        if not db:
            try:
                next(b)
            except StopIteration:
                db = True
        while not (da and db):
            for _ in range(ratio[1]):
                if db:
                    break
                try:
                    next(b)
                except StopIteration:
                    db = True
            for _ in range(ratio[0]):
                if da:
                    break
                try:
                    next(a)
                except StopIteration:
                    da = True

    def drain(a):
        for _ in a:
            pass

    def three_way(a, b, c):
        gens = [x for x in (a, b, c) if x is not None]
        while gens:
            for x in list(gens):
                try:
                    next(x)
                except StopIteration:
                    gens.remove(x)

    def fence_all():
        OP("dve", "tensor_copy", [bCST], get_alias() + [bufs["FDUM"]], out=FDUM[:, 0:1], in_=ZR[:, 0:1])

    SMALLH = [sb("SMALL0", [128, 64]), sb("SMALL1", [128, 64])]
    bSMALLH = [bufs["SMALL0"], bufs["SMALL1"]]
    TBH = [[TB[0], TB[1], TB[2]], [sb("TBX%d" % i, [128, 512], BF16) for i in range(3)]]
    bTBH = [[bTB[0], bTB[1], bTB[2]], [bufs["TBX%d" % i] for i in range(3)]]
    SQK2 = sb("SQK2", [128, 4096], BF16)
    KHATH = [KHAT, SQK2[:, :].rearrange("p (i a b) -> p i a b", i=8, a=4)]
    bKHATH = [bKHAT, bufs["SQK2"]]
    FQ = [sb("FQ%d" % i, [128, 512]) for i in range(3)]
    bFQ = [bufs["FQ%d" % i] for i in range(3)]
    EMH = [sb("EMH0", [128, 512]), sb("EMH1", [128, 512])]
    bEMH = [bufs["EMH0"], bufs["EMH1"]]
    MLX = [SQK2[:, :].bitcast(F32)[:, i * 512:(i + 1) * 512] for i in range(4)]
    bMLX = [Buf("mlx%d" % i) for i in range(4)]
    SST = sb("SST", [128, 128])
    SSTB = sb("SSTB", [128, 128], BF16)
    NREP = sb("NREP", [128, 128])
    NREPB = sb("NREPB", [128, 128], BF16)
    MS = sb("MS", [128, 96])
    bSST, bSSTB, bNREP, bNREPB, bMS = bufs["SST"], bufs["SSTB"], bufs["NREP"], bufs["NREPB"], bufs["MS"]
    YTOK = XTOK
    bYTOK = bXTOK
    WG = sb("WG", [128, 8, 16], BF16)
    bWG = bufs["WG"]
    out_bufs = []

    def rms_rstd(src_sq_ap_list, bsrc, inv_n, psum, bpsum):
        n = len(src_sq_ap_list)
        for i, a in enumerate(src_sq_ap_list):
            OP("pe", "matmul", [bONES] + bsrc, [bpsum], psum[:, :], lhsT=ONES[:], rhs=a, start=(i == 0), stop=(i == n - 1))
        OP("act", "activation", [bpsum, bKC], [bLNV], out=LNV[:], in_=psum[:, :], func=AF.Ln, bias=EPSC, scale=inv_n)
        OP("act", "activation", [bLNV], [bRSTD], out=RSTD[:], in_=LNV[:], func=AF.Exp, scale=-0.5)

    def norm_to_ht(job, l, wh, g, xg, bxg):
        j = job.cond
        gc = slice(g * 512, (g + 1) * 512)
        OP("act", "activation", [bxg], [bSQ], out=SQ[:].rearrange("p c n -> p (c n)"),
           in_=xg[:].rearrange("p c n -> p (c n)"), func=AF.Square)
        rms_rstd([SQ[:, c, :] for c in range(8)], [bSQ], 1.0 / D, PA, bPA)
        for c in range(8):
            OP("dve", "scalar_tensor_tensor", [bxg, bSC, bRSTD], [bxg], out=xg[:, c, :], in0=xg[:, c, :],
               scalar=SC[:, l, wh, c, j:j + 1], in1=RSTD[:], op0=ALU.mult, op1=ALU.mult)
            OP("act", "activation", [bxg, bMODS], [bHT[g]], out=HT[:, c, gc], in_=xg[:, c, :],
               func=AF.Identity, bias=modc(l, 0 if wh == 0 else 3, c, j), scale=1.0)

    pp = {"i": 0}

    def inproj(job, g, wv_fn, wbuf, ncols=128):
        i = pp["i"]
        pp["i"] ^= 1
        ps, bps = (PA, bPA) if i == 0 else (PB_, bPB)
        gc = slice(g * 512, (g + 1) * 512)
        for c in range(8):
            OP("pe", "matmul", [wbuf, bHT[g]], [bps], ps[0:ncols, :], lhsT=wv_fn(c), rhs=HT[:, c, gc],
               start=(c == 0), stop=(c == 7))
        return ps, bps

    def make_vtok(job, slot, sbuf_):
        wv = slot[:, 0:4096].rearrange("p (c n) -> p c n", c=8)
        for t in range(job.NT):
            i = pp["i"]
            pp["i"] ^= 1
            ps, bps = (PA, bPA) if i == 0 else (PB_, bPB)
            g = t // 4
            for c in range(8):
                OP("pe", "matmul", [sbuf_, bHT[g]], [bps], ps[:, :], lhsT=HT[:, c, t * 128:(t + 1) * 128], rhs=wv[:, c, :],
                   start=(c == 0), stop=(c == 7))
            if t % 2 == 0:
                OP("act", "copy", [bps], [bVTOK[g]], out=VTOK[:, t, :], in_=ps[:, :])
            else:
                OP("dve", "tensor_copy", [bps], [bVTOK[g]], out=VTOK[:, t, :], in_=ps[:, :])

    def head_finalize(job, l, g, gate_ps, bgate_ps, gate_func, acc_ap, bacc, normw_col, mix_head, temps=None):
        gc = slice(g * 512, (g + 1) * 512)
        if temps is None:
            temps = ((TMP[0], bTMP[0]), (TMP[2], bTMP[2]), (TMP[1], bTMP[1]))
        (tg, btg), (tsc, btsc), (t1, bt1) = temps
        sigm(gate_ps[:, :], [bgate_ps], tg[:], btg, tsc[:], btsc)
        if gate_func == AF.Silu:
            OP("dve", "tensor_tensor", [bgate_ps, btg], [btg], out=tg[:], in0=gate_ps[:, :], in1=tg[:], op=ALU.mult)
        yield
        OP("act", "activation", [bacc], [bTB[3]], out=TB[3][:], in_=acc_ap, func=AF.Square)
        rms_rstd([TB[3][:]], [bTB[3]], 1.0 / 128, PMISC, bPMISC)
        yield
        OP("dve", "scalar_tensor_tensor", [bacc, bV, bRSTD], [bt1], out=t1[:], in0=acc_ap, scalar=normw_col,
           in1=RSTD[:], op0=ALU.mult, op1=ALU.mult)
        OP("dve", "tensor_tensor", [bt1, btg], [bTB[5]], out=TB[5][:], in0=t1[:], in1=tg[:], op=ALU.mult)
        DMA("sp", "st_mix%d" % g, [bTB[5]], [bMIXD[job.name][g]], mixd[job.name][:, mix_head, gc], TB[5][:])
        yield

    def hg_head(job, l, h, slot, sbuf_):
        wv = slot[:, 0:4096].rearrange("p (c b n) -> p c b n", c=8, b=4)
        G = job.G
        SUB = 32
        NSB = 128 // SUB
        NR = 512 // SUB

        def qpass():
            for g in range(G):
                ps, bps = inproj(job, g, lambda c: wv[:, c, 0, :], sbuf_)
                sigm(ps[:, :], [bps], FQ[1][:], bFQ[1], FQ[2][:], bFQ[2])
                OP("dve", "tensor_tensor", [bps, bFQ[1]], [bQF[g]], out=QF[:, g * 512:(g + 1) * 512], in0=ps[:, :], in1=FQ[1][:], op=ALU.mult)
                yield

        units = [(0, g) for g in range(G)] + [(1, g) for g in range(G - 1, -1, -1)]
        pend = {}

        def issue_inproj(u):
            d_, g_ = units[u]
            pend[u] = inproj(job, g_, lambda c: wv[:, c, 1 + d_, :], sbuf_)

        def prep(d, g, hs, u):
            fwd = (d == 0)
            lbi = (l * 2 + d) * 4 + h
            gc0 = g * 512
            Qs, Qt, Kend = TBH[hs]
            bQs, bQt, bKend = bTBH[hs]
            KH, bKH = KHATH[hs], bKHATH[hs]
            SM, bSM = SMALLH[hs], bSMALLH[hs]
            if u not in pend:
                issue_inproj(u)
            ps, bps = pend.pop(u)
            t_e, t_lf, t_l2, t_k, t_B = TMP[0], TMP[1], TMP[2], TMP[3], TMP[4]
            b_lf = bTMP[1]
            OP("act", "activation", [bps], [bTMP[0]], out=t_e[:], in_=ps[:, :], func=AF.Exp, scale=-1.0)
            OP("act", "activation", [bTMP[0], bLB, bCST], [bTMP[1]], out=t_lf[:], in_=t_e[:], func=AF.Ln, bias=ONE, scale=LB[:, lbi:lbi + 1])
            OP("act", "activation", [bTMP[0], bCST], [bTMP[2]], out=t_l2[:], in_=t_e[:], func=AF.Ln, bias=ONE, scale=1.0)
            OP("act", "activation", [bTMP[2]], [bTMP[3]], out=t_k[:], in_=t_l2[:], func=AF.Exp, scale=-1.0)
            if u >= 1 and u + 1 < len(units):
                issue_inproj(u + 1)
            yield
            OP("dve", "tensor_tensor", [bTMP[1], bTMP[2]], [bTMP[1]], out=t_lf[:], in0=t_lf[:], in1=t_l2[:], op=ALU.subtract)
            OP("dve", "scalar_tensor_tensor", [bTMP[0], bOML, bTMP[3]], [bTMP[3]], out=t_k[:], in0=t_e[:], scalar=OML[:, lbi:lbi + 1],
               in1=t_k[:], op0=ALU.mult, op1=ALU.mult)
            if fwd:
                OP("dve", "tensor_tensor_scan", [bTMP[1], bCST], [bTMP[4]], out=t_B[:], data0=CST[:, C_RMF:C_RMF + 512],
                   data1=t_lf[:], initial=0.0, op0=ALU.mult, op1=ALU.add)
            else:
                OP("dve", "tensor_tensor_scan", [bTMP[1], bCST], [bTMP[4]], out=V3(t_B[:, 511:512], [[-1, 512]]),
                   data0=V3(CST[:, C_RMB + 511:C_RMB + 512], [[-1, 512]]),
                   data1=V3(t_lf[:, 511:512], [[-1, 512]]), initial=0.0, op0=ALU.mult, op1=ALU.add)
            e0 = 0 if fwd else SUB - 1
            el = 127 if fwd else 0
            R = SM[:, 0:NR]
            ATL = SM[:, 32:36]
            B3 = t_B[:].rearrange("p (a b) -> p a b", b=128)
            OP("dve", "tensor_tensor", [bTMP[4], bTMP[1]], [bSM], out=R, in0=V3(t_B[:, e0:e0 + 1], [[SUB, NR]]),
               in1=V3(t_lf[:, e0:e0 + 1], [[SUB, NR]]), op=ALU.subtract)
            OP("dve", "tensor_tensor", [bTMP[4], bSM], [bTMP[5]], out=TMP[5][:].rearrange("p (a b) -> p a b", b=SUB),
               in0=t_B[:].rearrange("p (a b) -> p a b", b=SUB), in1=V3(R[:, 0:1], [[1, NR], [0, SUB]]), op=ALU.subtract)
            OP("dve", "tensor_tensor", [bTMP[4]], [bTMP[0]], out=TMP[0][:].rearrange("p (a b) -> p a b", b=128),
               in0=V3(t_B[:, el:el + 1], [[128, 4], [0, 128]]), in1=B3, op=ALU.subtract)
            yield
            OP("act", "activation", [bTMP[5]], [bTMP[6]], out=TMP[6][:], in_=TMP[5][:], func=AF.Exp)
            OP("act", "activation", [bTMP[4]], [bTMP[7]], out=TMP[7][:], in_=t_B[:], func=AF.Exp)
            OP("act", "activation", [bTMP[4]], [bSM], out=ATL, in_=V3(t_B[:, el:el + 1], [[128, 4]]), func=AF.Exp)
            OP("act", "activation", [bTMP[0]], [bTMP[2]], out=TMP[2][:], in_=TMP[0][:], func=AF.Exp)

            def kh_sub(i):
                w = SUB * (i + 1) if fwd else 128
                ta, bta = TMP[8 + (i % 2) * 2], bTMP[8 + (i % 2) * 2]
                tav = ta[:].rearrange("p (a b) -> p a b", b=128)[:, :, 0:w]
                OP("dve", "tensor_tensor", [bTMP[4], bSM], [bta], out=tav, in0=V3(R[:, i:i + 1], [[NSB, 4], [0, w]]),
                   in1=B3[:, :, 0:w], op=ALU.subtract)

            def kh_act(i):
                w = SUB * (i + 1) if fwd else 128
                ta, te = TMP[8 + (i % 2) * 2], TMP[9 + (i % 2) * 2]
                bta, bte = bTMP[8 + (i % 2) * 2], bTMP[9 + (i % 2) * 2]
                tav = ta[:].rearrange("p (a b) -> p a b", b=128)[:, :, 0:w]
                tev = te[:].rearrange("p (a b) -> p a b", b=128)[:, :, 0:w]
                if not fwd:
                    OP("act", "activation", [bta, bCST], [bta], out=tav, in_=tav, func=AF.Relu, bias=C60, scale=-1.0)
                    OP("act", "activation", [bta, bCST], [bte], out=tev, in_=tav, func=AF.Exp, bias=C60, scale=-1.0)
                else:
                    OP("act", "activation", [bta], [bte], out=tev, in_=tav, func=AF.Exp)

            def kh_mul(i):
                w = SUB * (i + 1) if fwd else 128
                te, bte = TMP[9 + (i % 2) * 2], bTMP[9 + (i % 2) * 2]
                tev = te[:].rearrange("p (a b) -> p a b", b=128)[:, :, 0:w]
                OP("dve", "tensor_tensor", [bte, bTMP[3]], [bKH], out=KH[:, i, :, 0:w], in0=tev,
                   in1=t_k[:].rearrange("p (a b) -> p a b", b=128)[:, :, 0:w], op=ALU.mult)

            kh_sub(0)
            kh_sub(1)
            yield
            kh_act(0)
            kh_act(1)
            OP("dve", "tensor_tensor", [bTMP[6], bQF[g]], [bQs], out=Qs[:], in0=TMP[6][:], in1=QF[:, gc0:gc0 + 512], op=ALU.mult)
            OP("dve", "tensor_tensor", [bTMP[7], bQF[g]], [bQt], out=Qt[:], in0=TMP[7][:], in1=QF[:, gc0:gc0 + 512], op=ALU.mult)
            yield
            OP("dve", "tensor_tensor", [bTMP[2], bTMP[3]], [bKend], out=Kend[:], in0=TMP[2][:], in1=t_k[:], op=ALU.mult)
            for i in range(NSB):
                kh_mul(i)
                if i + 2 < NSB:
                    kh_sub(i + 2)
                    kh_act(i + 2)
                yield

        def main(d, g, hs):
            fwd = (d == 0)
            gc0 = g * 512
            Qs, Qt, Kend = TBH[hs]
            bQs, bQt, bKend = bTBH[hs]
            KH, bKH = KHATH[hs], bKHATH[hs]
            SM, bSM = SMALLH[hs], bSMALLH[hs]
            ATL = SM[:, 32:36]
            mask_bf = CB[:, 128:256] if fwd else CB[:, 256:384]
            for n in range(4):
                for i in range(NSB):
                    w = SUB * (i + 1) if fwd else 128
                    c0 = n * 128 + SUB * i
                    OP("pe", "matmul", [bKH, bQs], [bPAT], PAT[0:w, c0:c0 + SUB], lhsT=KH[:, i, n, 0:w],
                       rhs=Qs[:, c0:c0 + SUB], start=True, stop=True)
            for n in range(4):
                OP("pe", "transpose", [bKend, bCB], [PKTB], PKT[:, n * 128:(n + 1) * 128], Kend[:, n * 128:(n + 1) * 128], ident_bf)
            yield
            OP("dve", "tensor_tensor", [bPAT, bCB], [bTB[3]], out=TB[3][:].rearrange("p (a b) -> p a b", b=128),
               in0=PAT[:, :].rearrange("p (a b) -> p a b", b=128), in1=V3(mask_bf[:, 0:1], [[0, 4], [1, 128]]), op=ALU.mult)
            ATs = TB[3]
            OP("act", "copy", [PKTB], [bTB[4]], out=TB[4][:], in_=PKT[:, :])
            KT = TB[4]
            for n in range(4):
                t = g * 4 + n
                OP("pe", "matmul", [bTB[4], bVTOK[g]], [bPKV], PKV[:, n * 128:(n + 1) * 128], lhsT=KT[:, n * 128:(n + 1) * 128],
                   rhs=VTOK[:, t, h * 128:(h + 1) * 128], start=True, stop=True)
            yield
            norder = range(4) if fwd else range(3, -1, -1)
            for n in norder:
                t = g * 4 + n
                sq_, ts = t // job.tps, t % job.tps
                first = (ts == 0) if fwd else (ts == job.tps - 1)
                last = (ts == job.tps - 1) if fwd else (ts == 0)
                if first:
                    if job.init_cache:
                        DMA("sp", "ld_sst", [], [bSST], SST[:], st_hg[l, d, h, :, :])
                        OP("act", "copy", [bSST], [bSSTB], out=SSTB[:], in_=SST[:])
                    else:
                        OP("pool", "tensor_copy", [bCST], [bSST], out=SST[:], in_=ZR[:, 0:128])
                        OP("pool", "tensor_copy", [bCST], [bSSTB], out=SSTB[:], in_=ZR[:, 0:128])
                OP("pe", "matmul", [bTB[3], bVTOK[g]], [bPO], PO[:, n * 128:(n + 1) * 128], lhsT=VTOK[:, t, h * 128:(h + 1) * 128],
                   rhs=ATs[:, n * 128:(n + 1) * 128], start=True, stop=False)
                OP("pe", "matmul", [bSSTB, bQt], [bPO], PO[:, n * 128:(n + 1) * 128], lhsT=SSTB[:],
                   rhs=Qt[:, n * 128:(n + 1) * 128], start=False, stop=True)
                OP("dve", "scalar_tensor_tensor", [bSST, bSM, bPKV], [bSST], out=SST[:], in0=SST[:], scalar=ATL[:, n:n + 1],
                   in1=PKV[:, n * 128:(n + 1) * 128], op0=ALU.mult, op1=ALU.add)
                if last and job.state_out:
                    DMA("sp", "st_sst", [bSST], [], ns_hg[sq_, l, d, h, :, :], SST[:])
                if not last:
                    OP("act", "copy", [bSST], [bSSTB], out=SSTB[:], in_=SST[:])
                yield
            if fwd:
                OP("act", "copy", [bPO], [bOACC[g]], out=OACC[:, gc0:gc0 + 512], in_=PO[:, :])
            else:
                OP("dve", "tensor_tensor", [bPO, bOACC[g]], [bOACC[g]], out=OACC[:, gc0:gc0 + 512], in0=PO[:, :],
                   in1=OACC[:, gc0:gc0 + 512], op=ALU.add)
            yield

        def prep0():
            return prep(units[0][0], units[0][1], 0, 0)

        def run_units():
            for u, (d, g) in enumerate(units):
                nxt = prep(units[u + 1][0], units[u + 1][1], (u + 1) % 2, u + 1) if u + 1 < len(units) else None
                interleave(main(d, g, u % 2), nxt, ratio=(1, 1))

        def fin():
            ftemps = ((EMH[0], bEMH[0]), (EMH[1], bEMH[1]), (FQ[0], bFQ[0]))
            for g in range(G):
                ps, bps = inproj(job, g, lambda c: wv[:, c, 3, :], sbuf_)
                for _ in head_finalize(job, l, g, ps, bps, AF.Silu, OACC[:, g * 512:(g + 1) * 512], bOACC[g], vcol("hn", l * 4 + h), h,
                                       temps=ftemps):
                    yield

        return qpass, prep0, run_units, fin

    def conv_qk(job, l, ch, src, bsrc, dst_acc, eng):
        L = job.L
        def wc(a, b):
            return vcol("cw", ((l * 3 + a) * 3 + b) * 8 + ch)
        OP(eng, "tensor_scalar", bsrc + [bV], [bACC], out=dst_acc[:, 0:L], in0=src[:, 0:L], scalar1=wc(1, 1),
           scalar2=vcol("cb", l * 8 + ch), op0=ALU.mult, op1=ALU.add)
        yield
        if job.conv2d:
            Wd = 64
            R_ = L // Wd
            sv = src[:, 0:L].rearrange("p (r c) -> p r c", c=Wd)
            dv = dst_acc[:, 0:L].rearrange("p (r c) -> p r c", c=Wd)
            taps = [(a, b) for a in range(3) for b in range(3) if not (a == 1 and b == 1)]
            for (a, b) in taps:
                dr, dc = a - 1, b - 1
                r0, r1 = max(0, -dr), R_ - max(0, dr)
                c0, c1 = max(0, -dc), Wd - max(0, dc)
                OP(eng, "scalar_tensor_tensor", bsrc + [bV, bACC], [bACC], out=dv[:, r0:r1, c0:c1],
                   in0=sv[:, r0 + dr:r1 + dr, c0 + dc:c1 + dc], scalar=wc(a, b), in1=dv[:, r0:r1, c0:c1],
                   op0=ALU.mult, op1=ALU.add)
                yield
        else:
            Wd = job.seqlen
            sv = src[:, 0:L].rearrange("p (r c) -> p r c", c=Wd)
            dv = dst_acc[:, 0:L].rearrange("p (r c) -> p r c", c=Wd)
            for b in (0, 2):
                dc = b - 1
                c0, c1 = max(0, -dc), Wd - max(0, dc)
                OP(eng, "scalar_tensor_tensor", bsrc + [bV, bACC], [bACC], out=dv[:, :, c0:c1],
                   in0=sv[:, :, c0 + dc:c1 + dc], scalar=wc(1, b), in1=dv[:, :, c0:c1], op0=ALU.mult, op1=ALU.add)
                yield

    def ml_gates(job, l):
        DMA("sp", "ld_sel", [], [bSEL], SEL, selc[:, :])
        S.dma("pool", lambda e: e.dma_start(out=WG[:], in_=rows(w_in[l])[:, :, 4608:4624]), "ld_wg", [], [bWG])
        for g in range(job.G):
            ps, bps = inproj(job, g, lambda c: WG[:, c, :], bWG, ncols=16)
            OP("act", "activation", [bps, bufs["GB"]], [bG16[g]], out=G16[:, g * 512:(g + 1) * 512], in_=ps[0:16, :],
               func=AF.Identity, bias=GB[:, l:l + 1], scale=1.0)

    def ml_head(job, l, h, slot, sbuf_):
        wv = slot[:, 0:3072].rearrange("p (c b n) -> p c b n", c=8, b=3)
        G = job.G
        L = job.L
        def qpass():
            for qi in range(2):
                for g in range(G):
                    ps, bps = inproj(job, g, lambda c: wv[:, c, qi, :], sbuf_)
                    OP("act", "copy", [bps], [bPRE[g]], out=PRE[:, g * 512:(g + 1) * 512], in_=ps[:, :])
                    yield
                for _ in conv_qk(job, l, qi * 4 + h, PRE, bPRE[0:G], ACC, "dve"):
                    yield
                for g in range(G):
                    gs = slice(g * 512, (g + 1) * 512)
                    sigm(ACC[:, gs], [bACC], TMP[6][:], bTMP[6], TMP[7][:], bTMP[7])
                    if qi == 0:
                        OP("dve", "tensor_tensor", [bACC, bTMP[6]], [bQC[g]], out=QC[:, gs], in0=ACC[:, gs], in1=TMP[6][:], op=ALU.mult)
                    else:
                        OP("dve", "scalar_tensor_tensor", [bACC, bTMP[6]], [bKCV[g]], out=KCV[:, gs], in0=ACC[:, gs],
                           scalar=float(128 ** -0.5), in1=TMP[6][:], op0=ALU.mult, op1=ALU.mult)
                    yield

        msl = {"i": 0}

        def ms_next():
            msl["i"] = (msl["i"] + 1) % 96
            return MS[:, msl["i"]:msl["i"] + 1]

        HSETS = [([TMP[1], TMP[3], TMP[4], TMP[8]], [bTMP[1], bTMP[3], bTMP[4], bTMP[8]]), (MLX, bMLX)]
        chain = {"mstart": None}

        units = [(0, g) for g in range(G)] + [(1, g) for g in range(G - 1, -1, -1)]
        issued = set()

        def issue_bc(u):
            d_, g_ = units[u]
            ri_ = d_ * 4 + h
            rf_ = 8 + d_ * 4 + h
            c_ = g_ * 512
            OP("pe", "matmul", [bSEL, bG16[g_]], [bPA], PA[:, :], lhsT=SEL[:, ri_ * 128:(ri_ + 1) * 128], rhs=G16[:, c_:c_ + 512], start=True, stop=True)
            OP("pe", "matmul", [bSEL, bG16[g_]], [bPB], PB_[:, :], lhsT=SEL[:, rf_ * 128:(rf_ + 1) * 128], rhs=G16[:, c_:c_ + 512], start=True, stop=True)
            issued.add(u)

        def prep(d, g, hs, u):
            fwd = (d == 0)
            sidx = (l * 2 + d) * 4 + h
            gc0 = g * 512
            (t_b, t_M, t_ai, t_PTm), (b_b, b_M, b_ai, b_PTm) = HSETS[hs]
            SM, bSM = SMALLH[hs], bSMALLH[hs]
            mask_f = CST[:, C_MASKU:C_MASKU + 128] if fwd else CST[:, C_MASKL:C_MASKL + 128]
            if u not in issued:
                issue_bc(u)
            yield
            t_lf, t_w, t_PT = TMP[0], TMP[2], TMP[5]
            OP("act", "activation", [bPB], [bTMP[6]], out=TMP[6][:], in_=PB_[:, :], func=AF.Exp, scale=-1.0)
            OP("act", "activation", [bTMP[6], bCST], [bTMP[0]], out=t_lf[:], in_=TMP[6][:], func=AF.Ln, bias=ONE, scale=1.0)
            if fwd:
                OP("dve", "tensor_tensor_scan", [bTMP[0], bCST], [b_b], out=t_b[:], data0=CST[:, C_RMF:C_RMF + 512],
                   data1=t_lf[:], initial=0.0, op0=ALU.mult, op1=ALU.subtract)
            else:
                OP("dve", "tensor_tensor_scan", [bTMP[0], bCST], [b_b], out=V3(t_b[:, 511:512], [[-1, 512]]),
                   data0=V3(CST[:, C_RMB + 511:C_RMB + 512], [[-1, 512]]),
                   data1=V3(t_lf[:, 511:512], [[-1, 512]]), initial=0.0, op0=ALU.mult, op1=ALU.subtract)
            OP("dve", "tensor_tensor", [bPA, b_b], [bTMP[2]], out=t_w[:], in0=PA[:, :], in1=t_b[:], op=ALU.subtract)
            if u + 1 < len(units):
                issue_bc(u + 1)
            yield
            OP("dve", "tensor_tensor", [bTMP[2], bCST], [bTMP[7]], out=TMP[7][:].rearrange("p (a b) -> p a b", b=128),
               in0=t_w[:].rearrange("p (a b) -> p a b", b=128), in1=V3(ident[:, 0:1], [[0, 4], [1, 128]]), op=ALU.mult)
            WCOL = SM[:, 40:44]
            PEC = SM[:, 44:48]
            OP("dve", "tensor_reduce", [bTMP[7]], [bSM], out=WCOL, in_=TMP[7][:].rearrange("p (a b) -> p a b", b=128),
               axis=AX.X, op=ALU.add)
            yield
            norder = list(range(4)) if fwd else list(range(3, -1, -1))
            for n in norder:
                t = g * 4 + n
                sq_, ts = t // job.tps, t % job.tps
                first = (ts == 0) if fwd else (ts == job.tps - 1)
                last = (ts == job.tps - 1) if fwd else (ts == 0)
                if first:
                    chain["mstart"] = MB[:, sidx:sidx + 1] if job.init_cache else ZERO
                mstart = chain["mstart"]
                cs = n * 128
                if fwd:
                    OP("dve", "tensor_tensor_scan", [bTMP[2], bCST, bMS, bufs["MB"]], [b_M], out=t_M[:, cs:cs + 128],
                       data0=CST[:, C_NEGBIG:C_NEGBIG + 128], data1=t_w[:, cs:cs + 128], initial=mstart, op0=ALU.max, op1=ALU.max)
                    lc = cs + 127
                else:
                    OP("dve", "tensor_tensor_scan", [bTMP[2], bCST, bMS, bufs["MB"]], [b_M], out=V3(t_M[:, cs + 127:cs + 128], [[-1, 128]]),
                       data0=CST[:, C_NEGBIG:C_NEGBIG + 128], data1=V3(t_w[:, cs + 127:cs + 128], [[-1, 128]]),
                       initial=mstart, op0=ALU.max, op1=ALU.max)
                    lc = cs
                mnext = ms_next()
                OP("dve", "tensor_tensor", [b_b, b_M], [bMS], out=mnext, in0=t_b[:, lc:lc + 1], in1=t_M[:, lc:lc + 1], op=ALU.add)
                OP("act", "activation", [b_M, bMS, bufs["MB"], bCST], [b_ai], out=t_ai[:, cs:cs + 128], in_=t_M[:, cs:cs + 128],
                   func=AF.Exp, bias=mstart, scale=-1.0)
                OP("act", "activation", [b_M, bSM], [bTMP[5]], out=t_PT[:, cs:cs + 128], in_=t_M[:, cs:cs + 128],
                   func=AF.Exp, bias=WCOL[:, n:n + 1], scale=-1.0)
                if last and job.state_out:
                    DMA("sp", "st_m", [bMS], [], ns_m[sq_, l, d:d + 1, h:h + 1], mnext[0:1, :])
                chain["mstart"] = mnext
                yield
            OP("dve", "tensor_tensor", [b_b, b_M], [bTMP[7]], out=TMP[7][:], in0=t_b[:], in1=t_M[:], op=ALU.add)
            OP("act", "activation", [bTMP[7]], [bEMH[hs]], out=EMH[hs][:], in_=TMP[7][:], func=AF.Exp, scale=-1.0)
            yield
            lco = 127 if fwd else 0
            OP("act", "copy", [bTMP[5]], [bSM], out=PEC, in_=V3(t_PT[:, lco:lco + 1], [[128, 4]]))
            OP("pool", "tensor_tensor", [bTMP[5], bCST], [b_PTm], out=t_PTm[:].rearrange("p (a b) -> p a b", b=128),
               in0=t_PT[:].rearrange("p (a b) -> p a b", b=128), in1=V3(mask_f[:, 0:1], [[0, 4], [1, 128]]), op=ALU.mult)
            yield

        def main(d, g, hs):
            fwd = (d == 0)
            sidx = (l * 2 + d) * 4 + h
            gc0 = g * 512
            (t_b, t_M, t_ai, t_PTm), (b_b, b_M, b_ai, b_PTm) = HSETS[hs]
            SM, bSM = SMALLH[hs], bSMALLH[hs]
            PEC = SM[:, 44:48]
            for n in range(4):
                c0 = gc0 + n * 128
                OP("pe", "matmul", [bKCV[g], bQC[g]], [bPAT], PAT[:, n * 128:(n + 1) * 128], lhsT=KCV[:, c0:c0 + 128],
                   rhs=QC[:, c0:c0 + 128], start=True, stop=True)
            for n in range(4):
                c0 = gc0 + n * 128
                OP("pe", "transpose", [bKCV[g], bCB], [PKTB], PKT[:, n * 128:(n + 1) * 128], KCV[:, c0:c0 + 128], ident_bf)
            yield
            OP("dve", "tensor_tensor", [bPAT, b_PTm], [bTB[0]], out=TB[0][:], in0=PAT[:, :], in1=t_PTm[:], op=ALU.mult)
            ST = TB[0]
            OP("dve", "tensor_tensor", [bQC[g], b_ai], [bTB[1]], out=TB[1][:], in0=QC[:, gc0:gc0 + 512], in1=t_ai[:], op=ALU.mult)
            Qa = TB[1]
            for n in range(4):
                OP("act", "activation", [PKTB, bSM], [bTB[2]], out=TB[2][:, n * 128:(n + 1) * 128], in_=PKT[:, n * 128:(n + 1) * 128],
                   func=AF.Identity, scale=PEC[:, n:n + 1])
            KP = TB[2]
            yield
            for n in range(4):
                t = g * 4 + n
                OP("pe", "matmul", [bTB[2], bVTOK[g]], [bPKV], PKV[:, n * 128:(n + 1) * 128], lhsT=KP[:, n * 128:(n + 1) * 128],
                   rhs=VTOK[:, t, h * 128:(h + 1) * 128], start=True, stop=True)
                OP("pe", "matmul", [bTB[2], bONES], [bPMISC], PMISC[:, n * 128:(n + 1) * 128], lhsT=KP[:, n * 128:(n + 1) * 128],
                   rhs=ONES[:], start=True, stop=True)
            yield
            norder = list(range(4)) if fwd else list(range(3, -1, -1))
            for n in norder:
                t = g * 4 + n
                sq_, ts = t // job.tps, t % job.tps
                first = (ts == 0) if fwd else (ts == job.tps - 1)
                last = (ts == job.tps - 1) if fwd else (ts == 0)
                cs = n * 128
                if first:
                    if job.init_cache:
                        DMA("sp", "ld_sst", [], [bSST], SST[:], st_c[l, d, h, :, :])
                        OP("act", "copy", [bSST], [bSSTB], out=SSTB[:], in_=SST[:])
                        OP("dve", "tensor_copy", [bV], [bNREP], out=NREP[:], in_=V3(vcol("n0", sidx), [[0, 128]]))
                        OP("act", "copy", [bNREP], [bNREPB], out=NREPB[:], in_=NREP[:])
                    else:
                        OP("pool", "tensor_copy", [bCST], [bSST], out=SST[:], in_=ZR[:, 0:128])
                        OP("pool", "tensor_copy", [bCST], [bSSTB], out=SSTB[:], in_=ZR[:, 0:128])
                        OP("pool", "tensor_copy", [bCST], [bNREP], out=NREP[:], in_=ZR[:, 0:128])
                        OP("pool", "tensor_copy", [bCST], [bNREPB], out=NREPB[:], in_=ZR[:, 0:128])
                OP("pe", "matmul", [bTB[0], bVTOK[g]], [bPO], PO[:, cs:cs + 128], lhsT=VTOK[:, t, h * 128:(h + 1) * 128],
                   rhs=ST[:, cs:cs + 128], start=True, stop=False)
                OP("pe", "matmul", [bSSTB, bTB[1]], [bPO], PO[:, cs:cs + 128], lhsT=SSTB[:], rhs=Qa[:, cs:cs + 128], start=False, stop=True)
                OP("pe", "matmul", [bTB[0], bONES], [bPDEN], PDEN[:, cs:cs + 128], lhsT=ONES[:],
                   rhs=ST[:, cs:cs + 128], start=True, stop=False)
                OP("pe", "matmul", [bNREPB, bTB[1]], [bPDEN], PDEN[:, cs:cs + 128], lhsT=NREPB[:], rhs=Qa[:, cs:cs + 128], start=False, stop=True)
                lc = cs + 127 if fwd else cs
                OP("dve", "scalar_tensor_tensor", [bSST, b_ai, bPKV], [bSST], out=SST[:], in0=SST[:], scalar=t_ai[:, lc:lc + 1],
                   in1=PKV[:, cs:cs + 128], op0=ALU.mult, op1=ALU.add)
                OP("dve", "scalar_tensor_tensor", [bNREP, b_ai, bPMISC], [bNREP], out=NREP[:], in0=NREP[:], scalar=t_ai[:, lc:lc + 1],
                   in1=PMISC[:, cs:cs + 128], op0=ALU.mult, op1=ALU.add)
                if last and job.state_out:
                    DMA("sp", "st_sst", [bSST], [], ns_c[sq_, l, d, h, :, :], SST[:])
                    DMA("sp", "st_nrep", [bNREP], [], ns_n[sq_, l, d, h, :].rearrange("(p o) -> p o", o=1), NREP[:, 0:1])
                if not last:
                    OP("act", "copy", [bSST], [bSSTB], out=SSTB[:], in_=SST[:])
                    OP("act", "copy", [bNREP], [bNREPB], out=NREPB[:], in_=NREP[:])
                yield
            OP("act", "copy", [bPDEN], [bTMP[11]], out=TMP[11][:], in_=PDEN[:, :])
            yield
            OP("dve", "scalar_tensor_tensor", [bTMP[11]], [bTMP[9]], out=TMP[9][:], in0=TMP[11][:], scalar=-1.0, in1=TMP[11][:],
               op0=ALU.mult, op1=ALU.max)
            OP("dve", "tensor_tensor", [bTMP[9], bEMH[hs]], [bTMP[9]], out=TMP[9][:], in0=TMP[9][:], in1=EMH[hs][:], op=ALU.max)
            OP("act", "activation", [bTMP[9]], [bTMP[10]], out=TMP[10][:], in_=TMP[9][:], func=AF.Ln)
            OP("act", "activation", [bTMP[10]], [bTMP[10]], out=TMP[10][:], in_=TMP[10][:], func=AF.Exp, scale=-1.0)
            yield
            if fwd:
                OP("dve", "tensor_tensor", [bPO, bTMP[10]], [bHACC[g]], out=HACC[:, gc0:gc0 + 512], in0=PO[:, :], in1=TMP[10][:], op=ALU.mult)
            else:
                OP("dve", "tensor_tensor", [bPO, bTMP[10]], [bTMP[11]], out=TMP[11][:], in0=PO[:, :], in1=TMP[10][:], op=ALU.mult)
                OP("pool", "tensor_tensor", [bTMP[11], bHACC[g]], [bHACC[g]], out=HACC[:, gc0:gc0 + 512], in0=TMP[11][:],
                   in1=HACC[:, gc0:gc0 + 512], op=ALU.add)
            yield

        def run_units():
            drain(prep(units[0][0], units[0][1], 0, 0))
            for u, (d, g) in enumerate(units):
                nxt = prep(units[u + 1][0], units[u + 1][1], (u + 1) % 2, u + 1) if u + 1 < len(units) else None
                interleave(main(d, g, u % 2), nxt)

        def fin():
            for g in range(G):
                ps, bps = inproj(job, g, lambda c: wv[:, c, 2, :], sbuf_)
                for _ in head_finalize(job, l, g, ps, bps, AF.Sigmoid, HACC[:, g * 512:(g + 1) * 512], bHACC[g], vcol("mn", l * 4 + h), 4 + h):
                    yield

        return qpass, run_units, fin

    def ffn(job, l):
        j = job.cond
        PSS = [(PA, bPA), (PB_, bPB), (PAT, bPAT), (PKV, bPKV)]
        for tg in range(job.L // 1024):
            k = 0
            for b in range(6):
                ncol = 512 if b < 5 else 256
                sg, bsg = w_take()
                su, bsu = w_take()
                gv = sg[:, 0:8 * ncol].rearrange("p (c n) -> p c n", c=8)
                uv = su[:, 0:8 * ncol].rearrange("p (c n) -> p c n", c=8)
                for fcl in range(ncol // 128):
                    fc = b * 4 + fcl
                    for sub in range(2):
                        g = tg * 2 + sub
                        gc = slice(g * 512, (g + 1) * 512)
                        (p1, bp1), (p2, bp2) = PSS[(k % 2) * 2], PSS[(k % 2) * 2 + 1]
                        k += 1
                        for c in range(8):
                            OP("pe", "matmul", [bsg, bHT[g]], [bp1], p1[:, :], lhsT=gv[:, c, fcl * 128:(fcl + 1) * 128], rhs=HT[:, c, gc],
                               start=(c == 0), stop=(c == 7))
                        for c in range(8):
                            OP("pe", "matmul", [bsu, bHT[g]], [bp2], p2[:, :], lhsT=uv[:, c, fcl * 128:(fcl + 1) * 128], rhs=HT[:, c, gc],
                               start=(c == 0), stop=(c == 7))
                        tt, btt = TMP[k % 4], bTMP[k % 4]
                        OP("act", "activation", [bp1], [btt], out=tt[:], in_=p1[:, :], func=AF.Silu)
                        OP("dve", "tensor_tensor", [btt, bp2], [bABUF[sub]], out=ABUF[:, fc * 1024 + sub * 512:fc * 1024 + (sub + 1) * 512],
                           in0=tt[:], in1=p2[:, :], op=ALU.mult)
            for sub in range(2):
                g = tg * 2 + sub
                DMA("sp", "ld_xg%d" % sub, [], [bXG[sub]], XG[sub][:], xd[job.name][:, :, g * 512:(g + 1) * 512])
            for dc in range(8):
                sd, bsd = w_take()
                dv = sd[:, 0:NFF * 128].rearrange("p (f n) -> p f n", f=NFF)
                for sub in range(2):
                    p1, bp1 = PSS[(dc * 2 + sub) % 4]
                    for fc in range(NFF):
                        OP("pe", "matmul", [bsd, bABUF[sub]], [bp1], p1[:, :], lhsT=dv[:, fc, :],
                           rhs=ABUF[:, fc * 1024 + sub * 512:fc * 1024 + (sub + 1) * 512], start=(fc == 0), stop=(fc == NFF - 1))
                    OP("dve", "scalar_tensor_tensor", [bp1, bMODS, bXG[sub]], [bXG[sub]], out=XG[sub][:, dc, :], in0=p1[:, :],
                       scalar=modc(l, 5, dc, j), in1=XG[sub][:, dc, :], op0=ALU.mult, op1=ALU.add)
            for sub in range(2):
                g = tg * 2 + sub
                DMA("sp", "st_xg%d" % sub, [bXG[sub]], [bXD[job.name][g]], xd[job.name][:, :, g * 512:(g + 1) * 512], XG[sub][:])

    bXD = {"S": [Buf("xdS%d" % g) for g in range(4)], "P": [Buf("xdP%d" % g) for g in range(2)]}

    for job in jobs:
        fence_all()
        for t in range(job.NT):
            g, n = t // 4, t % 4
            xt, bxt = XTOK[t % 2], bXTOK[t % 2]
            xg, bxg = XG[g % 2], bXG[g % 2]
            DMA("sp", "ld_xtok%d" % (t % 2), [], [bxt], xt[:], xin[job.name][t * 128:(t + 1) * 128, :])
            for c in range(8):
                ps, bps = (PA, bPA) if c < 4 else (PB_, bPB)
                OP("pe", "transpose", [bxt, bCST], [bps], ps[:, (c % 4) * 128:(c % 4 + 1) * 128], xt[:, c * 128:(c + 1) * 128], ident)
            OP("act", "copy", [bPA], [bxg], out=xg[:, 0:4, n * 128:(n + 1) * 128], in_=PA[:, :].rearrange("p (c n) -> p c n", c=4))
            OP("dve", "tensor_copy", [bPB], [bxg], out=xg[:, 4:8, n * 128:(n + 1) * 128], in_=PB_[:, :].rearrange("p (c n) -> p c n", c=4))
            if n == 3:
                DMA("sp", "st_xg%d" % (g % 2), [bxg], [bXD[job.name][g]], xd[job.name][:, :, g * 512:(g + 1) * 512], xg[:])
        for l in range(nlayers):
            fence_all()
            for g in range(job.G):
                xg, bxg = XG[g % 2], bXG[g % 2]
                DMA("sp", "ld_xg%d" % (g % 2), [bXD[job.name][g]], [bxg], xg[:], xd[job.name][:, :, g * 512:(g + 1) * 512])
                norm_to_ht(job, l, 0, g, xg, bxg)
            fence_all()
            if do_hg:
                slot, sbuf_ = w_take()
                make_vtok(job, slot, sbuf_)
                prev_fin = None
                for h in range(4):
                    slot, sbuf_ = w_take()
                    qp_, p0_, ru, fn_ = hg_head(job, l, h, slot, sbuf_)
                    qp = qp_()
                    next(qp)
                    three_way(prev_fin, qp, p0_())
                    ru()
                    prev_fin = fn_()
                drain(prev_fin)
            fence_all()
            if do_ml:
                ml_gates(job, l)
                slot, sbuf_ = w_take()
                make_vtok(job, slot, sbuf_)
                prev_fin = None
                for h in range(4):
                    slot, sbuf_ = w_take()
                    qp, ru, fn_ = ml_head(job, l, h, slot, sbuf_)
                    drain(qp())
                    ru()
                    drain(fn_())
            fence_all()
            if do_hg or do_ml:
                s0, bs0 = w_take()
                s1, bs1 = w_take()
                wo = [s0[:, 0:4096].rearrange("p (c n) -> p c n", c=8), s1[:, 0:4096].rearrange("p (c n) -> p c n", c=8)]
                bwo = [bs0, bs1]
                e_list = [e for e in range(8) if (do_hg and e < 4) or (do_ml and e >= 4)]
                for g in range(job.G):
                    gc = slice(g * 512, (g + 1) * 512)
                    xg, bxg = XG[g % 2], bXG[g % 2]
                    DMA("sp", "ld_xg%d" % (g % 2), [bXD[job.name][g]], [bxg], xg[:], xd[job.name][:, :, gc])
                    DMA("sp", "ld_mixg", [bMIXD[job.name][g]], [bMIXG], MIXG[:], mixd[job.name][:, :, gc])
                    for dc in range(8):
                        ps, bps = (PA, bPA) if dc % 2 == 0 else (PB_, bPB)
                        for ei, e in enumerate(e_list):
                            OP("pe", "matmul", [bwo[dc // 4], bMIXG], [bps], ps[:, :], lhsT=wo[dc // 4][:, e, (dc % 4) * 128:(dc % 4 + 1) * 128],
                               rhs=MIXG[:, e, :], start=(ei == 0), stop=(ei == len(e_list) - 1))
                        OP("dve", "scalar_tensor_tensor", [bps, bMODS, bxg], [bxg], out=xg[:, dc, :], in0=ps[:, :],
                           scalar=modc(l, 2, dc, job.cond), in1=xg[:, dc, :], op0=ALU.mult, op1=ALU.add)
                    DMA("sp", "st_xg%d" % (g % 2), [bxg], [bXD[job.name][g]], xd[job.name][:, :, gc], xg[:])
                    if do_ffn:
                        norm_to_ht(job, l, 1, g, xg, bxg)
            elif do_ffn:
                for g in range(job.G):
                    xg, bxg = XG[g % 2], bXG[g % 2]
                    DMA("sp", "ld_xg%d" % (g % 2), [bXD[job.name][g]], [bxg], xg[:], xd[job.name][:, :, g * 512:(g + 1) * 512])
                    norm_to_ht(job, l, 1, g, xg, bxg)
            fence_all()
            if do_ffn:
                ffn(job, l)
        fence_all()
        fo = voff["fn"]
        for g in range(job.G):
            xg, bxg = XG[g % 2], bXG[g % 2]
            DMA("sp", "ld_xg%d" % (g % 2), [bXD[job.name][g]], [bxg], xg[:], xd[job.name][:, :, g * 512:(g + 1) * 512])
            OP("act", "activation", [bxg], [bSQ], out=SQ[:].rearrange("p c n -> p (c n)"),
               in_=xg[:].rearrange("p c n -> p (c n)"), func=AF.Square)
            rms_rstd([SQ[:, c, :] for c in range(8)], [bSQ], 1.0 / D, PMISC, bPMISC)
            for c in range(8):
                OP("dve", "scalar_tensor_tensor", [bxg, bV, bRSTD], [bxg], out=xg[:, c, :], in0=xg[:, c, :],
                   scalar=V[:, fo + c:fo + c + 1], in1=RSTD[:], op0=ALU.mult, op1=ALU.mult)
            for n in range(4):
                t = g * 4 + n
                yt, byt = YTOK[t % 2], bYTOK[t % 2]
                for c in range(8):
                    ps, bps = (PA, bPA) if c < 4 else (PB_, bPB)
                    OP("pe", "transpose", [bxg, bCST], [bps], ps[:, (c % 4) * 128:(c % 4 + 1) * 128], xg[:, c, n * 128:(n + 1) * 128], ident)
                OP("act", "copy", [bPA], [byt], out=yt[:, 0:512], in_=PA[:, :])
                OP("dve", "tensor_copy", [bPB], [byt], out=yt[:, 512:1024], in_=PB_[:, :])
                DMA("sp", "st_ytok%d" % (t % 2), [byt], [], yout[job.name][t * 128:(t + 1) * 128, :], yt[:])

    allb = list(bufs.values()) + PSB + [PKTB] + bHT + get_alias() + bXD["S"] + bXD["P"] + bMIXD["S"] + bMIXD["P"]
    S.finish("sp", allb)
    S.emit(st)
    st.close()
    return nc, S


_CACHE = {}


def _f32(a):
    return np.ascontiguousarray(np.asarray(a), dtype=np.float32)


def kernel(x_prompt, x_sample, state_hgrn, state_mlstm_c, state_mlstm_n, state_mlstm_m, c, c_ctx,
           norm1_w, norm2_w, w_mod, b_mod, w_in, conv_w, conv_b, ml_gate_b, hg_lb_logits,
           hg_norm_w, ml_norm_w, w_out, w_gate, w_up, w_down, final_norm_w, _opts=None):
    opts = _opts or {}
    key = tuple(sorted(opts.items()))
    if key not in _CACHE:
        _CACHE[key] = build_program(**opts)
    nc, _ = _CACHE[key]
    cst, sel = host_consts()
    shared = {
        "norm1_w": _f32(norm1_w), "norm2_w": _f32(norm2_w), "w_mod": _f32(w_mod), "b_mod": _f32(b_mod),
        "w_in": _f32(w_in), "conv_w": _f32(conv_w), "conv_b": _f32(conv_b), "ml_gate_b": _f32(ml_gate_b),
        "hg_lb_logits": _f32(hg_lb_logits), "hg_norm_w": _f32(hg_norm_w), "ml_norm_w": _f32(ml_norm_w),
        "w_out": _f32(w_out), "w_gate": _f32(w_gate), "w_up": _f32(w_up), "w_down": _f32(w_down),
        "final_norm_w": _f32(final_norm_w), "consts": cst, "selc": sel,
    }
    x_prompt = _f32(x_prompt)
    x_sample = _f32(x_sample)
    c = _f32(c)
    c_ctx = _f32(c_ctx)
    in_maps = []
    for i in range(8):
        m = dict(shared)
        m["x_s"] = x_sample[i]
        m["x_p"] = x_prompt[4 * i:4 * i + 4].reshape(1024, D)
        m["cvec"] = np.concatenate([c[i].reshape(8, 128), c_ctx.reshape(8, 128)], axis=0)
        m["st_hg"] = _f32(state_hgrn[i])
        m["st_c"] = _f32(state_mlstm_c[i])
        m["st_n"] = _f32(state_mlstm_n[i]).reshape(32, 128)
        m["st_m"] = _f32(state_mlstm_m[i]).reshape(1, 32)
        in_maps.append(m)
    res = run_bass_kernel_spmd(nc, in_maps, core_ids=list(range(8)))
    rs = res.results
    y_prompt = np.concatenate([r["y_p"].reshape(4, 256, D) for r in rs], axis=0)
    y_sample = np.stack([r["y_s"] for r in rs], axis=0)
    ns_hg = np.concatenate([r["ns_hg"] for r in rs], axis=0)
    ns_c = np.concatenate([r["ns_c"] for r in rs], axis=0)
    ns_n = np.concatenate([r["ns_n"] for r in rs], axis=0)
    ns_m = np.concatenate([r["ns_m"] for r in rs], axis=0)
    return (y_prompt.astype(np.float32), y_sample.astype(np.float32), ns_hg.astype(np.float32),
            ns_c.astype(np.float32), ns_n.astype(np.float32), ns_m.astype(np.float32))
```

```python
import numpy as np
from contextlib import ExitStack
import concourse.bass as bass
import concourse.mybir as mybir
from concourse.bass_utils import run_bass_kernel_spmd

F32 = mybir.dt.float32
BF16 = mybir.dt.bfloat16
ALU = mybir.AluOpType
AF = mybir.ActivationFunctionType
AX = mybir.AxisListType

ENGS = ["pe", "act", "dve", "pool", "sp"]
DEPTH = 4
D = 1024
DFF = 2816
NFF = 22
DIN = 4624
EPS = 1e-6


class Buf:
    __slots__ = ("name", "last_w", "readers")

    def __init__(self, name):
        self.name = name
        self.last_w = None
        self.readers = []


class Sched:
    def __init__(self, nc):
        self.nc = nc
        self.ops = {e: [] for e in ENGS}
        self.seen = {e: {} for e in ENGS}
        self.dma_cnt = {}
        self.sig = set()
        self.final_waits = []

    def _deps(self, eng, reads, writes):
        deps = []
        for b in reads:
            if b.last_w is not None:
                deps.append(b.last_w)
        for b in writes:
            if b.last_w is not None:
                deps.append(b.last_w)
            deps.extend(b.readers)
        seen = self.seen[eng]
        best = {}
        for ev in deps:
            if ev[0] == "c" and ev[1] == "pe" and eng == "pe":
                continue
            key = (ev[0], ev[1])
            if seen.get(key, -1) >= ev[2]:
                continue
            if key not in best or best[key][2] < ev[2]:
                best[key] = ev
        for key, ev in best.items():
            seen[key] = ev[2]
            if ev[0] == "c":
                self.sig.add((ev[1], ev[2]))
        return list(best.values())

    def op(self, eng, fn, reads=(), writes=()):
        waits = self._deps(eng, reads, writes)
        idx = len(self.ops[eng])
        ev = ("c", eng, idx)
        self.ops[eng].append((waits, fn, None))
        for b in reads:
            b.readers.append(ev)
        for b in writes:
            b.last_w = ev
            b.readers = []
        return ev

    def dma(self, eng, fn, key, reads=(), writes=(), n=1):
        waits = self._deps(eng, reads, writes)
        val = self.dma_cnt.get(key, 0) + 16 * n
        self.dma_cnt[key] = val
        ev = ("d", key, val)
        self.ops[eng].append((waits, fn, key))
        for b in reads:
            b.readers.append(ev)
        for b in writes:
            b.last_w = ev
            b.readers = []
        return ev

    def finish(self, eng, bufs):
        evs = []
        for b in bufs:
            if b.last_w is not None:
                evs.append(b.last_w)
            evs.extend(b.readers)
        self.final_waits.append((eng, evs))
        for ev in evs:
            if ev[0] == "c":
                self.sig.add((ev[1], ev[2]))

    def emit(self, stack):
        nc = self.nc
        sems = {e: stack.enter_context(nc.semaphore("s_" + e)) for e in ENGS}
        dsems = {k: stack.enter_context(nc.semaphore("d_" + str(k))) for k in self.dma_cnt}
        rank = {}
        for e in ENGS:
            r = 0
            for i, o in enumerate(self.ops[e]):
                if o[2] is None and (e, i) in self.sig:
                    r += 1
                    rank[(e, i)] = r

        def run(e, engine):
            def w(ev):
                if ev[0] == "c":
                    engine.wait_ge(sems[ev[1]], rank[(ev[1], ev[2])])
                else:
                    engine.wait_ge(dsems[ev[1]], ev[2])
            for i, (waits, fn, dkey) in enumerate(self.ops[e]):
                for ev in waits:
                    w(ev)
                if dkey is not None:
                    res = fn(engine)
                    if not isinstance(res, (list, tuple)):
                        res = [res]
                    for ins in res:
                        ins.then_inc(dsems[dkey], 16)
                else:
                    ins = fn(engine)
                    if (e, i) in rank:
                        ins.then_inc(sems[e], 1)
            for (fe, evs) in self.final_waits:
                if fe == e:
                    best = {}
                    for ev in evs:
                        k = (ev[0], ev[1])
                        if k not in best or best[k][2] < ev[2]:
                            best[k] = ev
                    for ev in best.values():
                        w(ev)

        block = stack.enter_context(nc.Block())

        @block.tensor
        def _(eng):
            run("pe", eng)

        @block.scalar
        def _(eng):
            run("act", eng)

        @block.vector
        def _(eng):
            run("dve", eng)

        @block.gpsimd
        def _(eng):
            run("pool", eng)

        @block.sync
        def _(eng):
            run("sp", eng)


def V3(ap, dims):
    return bass.AP(ap.tensor, ap.offset, [list(ap.ap[0])] + [list(d) for d in dims])


C_IDENT = 0
C_MASKU = 128
C_MASKL = 256
C_RMF = 384
C_RMB = 896
C_NEGBIG = 1408
C_ONES = 1536
C_ZERO = 1664
C_KC = 2176
NCONST = 2180


def host_consts():
    c = np.zeros((128, NCONST), np.float32)
    c[:, C_IDENT:C_IDENT + 128] = np.eye(128, dtype=np.float32)
    s = np.arange(128)[:, None]
    t = np.arange(128)[None, :]
    c[:, C_MASKU:C_MASKU + 128] = (s <= t).astype(np.float32)
    c[:, C_MASKL:C_MASKL + 128] = (s >= t).astype(np.float32)
    tt = np.arange(512)
    c[:, C_RMF:C_RMF + 512] = (tt % 128 != 0).astype(np.float32)[None, :]
    c[:, C_RMB:C_RMB + 512] = (tt % 128 != 127).astype(np.float32)[None, :]
    c[:, C_NEGBIG:C_NEGBIG + 128] = -1e30
    c[:, C_ONES:C_ONES + 128] = 1.0
    c[:, C_KC + 1] = 1.0
    c[:, C_KC + 2] = EPS
    c[:, C_KC + 3] = 75.0
    sel = np.zeros((16, 16 * 128), np.float32)
    for r in range(16):
        sel[r, r * 128:(r + 1) * 128] = 1.0
    return c, sel


class Job:
    def __init__(self, name, L, seqlen, cond, conv2d, init_cache, state_out):
        self.name = name
        self.L = L
        self.seqlen = seqlen
        self.nseq = L // seqlen
        self.G = L // 512
        self.NT = L // 128
        self.tps = seqlen // 128
        self.cond = cond
        self.conv2d = conv2d
        self.init_cache = init_cache
        self.state_out = state_out


def build_program(nlayers=DEPTH, do_hg=True, do_ml=True, do_ffn=True):
    nc = bass.Bass("TRN2", target_bir_lowering=False)
    dt_in = lambda name, shape: nc.dram_tensor(name, list(shape), F32, kind="ExternalInput").ap()
    dt_out = lambda name, shape: nc.dram_tensor(name, list(shape), F32, kind="ExternalOutput").ap()

    x_s = dt_in("x_s", (2048, D))
    x_p = dt_in("x_p", (1024, D))
    cvec = dt_in("cvec", (16, 128))
    st_hg = dt_in("st_hg", (4, 2, 4, 128, 128))
    st_c = dt_in("st_c", (4, 2, 4, 128, 128))
    st_n = dt_in("st_n", (32, 128))
    st_m = dt_in("st_m", (1, 32))
    norm1_w = dt_in("norm1_w", (4, D))
    norm2_w = dt_in("norm2_w", (4, D))
    w_mod = dt_in("w_mod", (4, D, 6 * D))
    b_mod = dt_in("b_mod", (4, 6 * D))
    w_in = dt_in("w_in", (4, D, DIN))
    conv_w = dt_in("conv_w", (4, 3, 3, D))
    conv_b = dt_in("conv_b", (4, D))
    ml_gate_b = dt_in("ml_gate_b", (4, 16))
    hg_lb_logits = dt_in("hg_lb_logits", (4, 2, 512))
    hg_norm_w = dt_in("hg_norm_w", (4, 512))
    ml_norm_w = dt_in("ml_norm_w", (4, 512))
    w_out = dt_in("w_out", (4, D, D))
    w_gate = dt_in("w_gate", (4, D, DFF))
    w_up = dt_in("w_up", (4, D, DFF))
    w_down = dt_in("w_down", (4, DFF, D))
    final_norm_w = dt_in("final_norm_w", (D,))
    consts = dt_in("consts", (128, NCONST))
    selc = dt_in("selc", (16, 2048))

    y_s = dt_out("y_s", (2048, D))
    y_p = dt_out("y_p", (1024, D))
    ns_hg = dt_out("ns_hg", (4, 4, 2, 4, 128, 128))
    ns_c = dt_out("ns_c", (4, 4, 2, 4, 128, 128))
    ns_n = dt_out("ns_n", (4, 4, 2, 4, 128))
    ns_m = dt_out("ns_m", (4, 4, 2, 4))

    xd_s = nc.dram_tensor("xd_s", [128, 8, 2048], F32).ap()
    xd_p = nc.dram_tensor("xd_p", [128, 8, 1024], F32).ap()

    jobs = [Job("S", 2048, 2048, 0, True, True, False),
            Job("P", 1024, 256, 1, False, False, True)]
    xin = {"S": x_s, "P": x_p}
    xd = {"S": xd_s, "P": xd_p}
    yout = {"S": y_s, "P": y_p}

    S = Sched(nc)
    st = ExitStack()

    bufs = {}

    def sb(name, shape, dt=F32):
        t = st.enter_context(nc.sbuf_tensor(name, list(shape), dt))
        bufs[name] = Buf(name)
        return t

    def OP(eng, method, reads, writes, *args, **kw):
        return S.op(eng, lambda e: getattr(e, method)(*args, **kw), reads, writes)

    def DMA(eng, key, reads, writes, out, in_):
        return S.dma(eng, lambda e: e.dma_start(out=out, in_=in_), key, reads, writes)

    def sigm(src, bsrc, ta, bta, tb, btb):
        OP("act", "activation", bsrc, [bta], out=ta, in_=src, func=AF.Exp, scale=-1.0)
        OP("act", "activation", [bta, bCSTh[0]], [btb], out=tb, in_=ta, func=AF.Ln, bias=ONEh[0], scale=1.0)
        OP("act", "activation", [btb], [bta], out=ta, in_=tb, func=AF.Exp, scale=-1.0)

    bCSTh = [None]
    ONEh = [None]
    PS = []
    PSB = []
    for i in range(7):
        PS.append(st.enter_context(nc.psum_tensor("ps%d" % i, [128, 512], F32)))
        PSB.append(Buf("ps%d" % i))
    PKT = st.enter_context(nc.psum_tensor("pkt", [128, 512], BF16))
    PKTB = Buf("pkt")
    PA, PB_, PAT, PKV, PO, PDEN, PMISC = PS
    bPA, bPB, bPAT, bPKV, bPO, bPDEN, bPMISC = PSB

    CST = sb("CST", [128, NCONST])
    DMA("sp", "ld_cst", [], [bufs["CST"]], CST[:], consts[:, :])
    ident = CST[:, C_IDENT:C_IDENT + 128]
    CB = sb("CB", [128, 3 * 128], BF16)
    OP("dve", "tensor_copy", [bufs["CST"]], [bufs["CB"]], out=CB[:], in_=CST[:, 0:384])
    ident_bf = CB[:, 0:128]
    ONES = sb("ONES", [128, 128], BF16)
    OP("dve", "tensor_copy", [bufs["CST"]], [bufs["ONES"]], out=ONES[:], in_=CST[:, C_ONES:C_ONES + 128])
    KC = CST[:, C_KC:C_KC + 4]
    ZERO = KC[:, 0:1]
    ONE = KC[:, 1:2]
    EPSC = KC[:, 2:3]
    C60 = KC[:, 3:4]
    bCST = bufs["CST"]
    bKC = bCST
    bCB = bufs["CB"]
    bONES = bufs["ONES"]
    ZR = CST[:, C_ZERO:C_ZERO + 512]
    bCSTh[0] = bCST
    ONEh[0] = ONE
    for i in range(7):
        OP("dve", "tensor_copy", [bCST], [PSB[i]], out=PS[i][:], in_=ZR)

    vec_src = [
        ("n1", norm1_w.rearrange("l (c p) -> (l c) p", p=128)),
        ("n2", norm2_w.rearrange("l (c p) -> (l c) p", p=128)),
        ("fn", final_norm_w.rearrange("(c p) -> c p", p=128)),
        ("bm", b_mod.rearrange("l (c p) -> (l c) p", p=128)),
        ("cw", conv_w.rearrange("l a b (c p) -> (l a b c) p", p=128)),
        ("cb", conv_b.rearrange("l (c p) -> (l c) p", p=128)),
        ("lb", hg_lb_logits.rearrange("l d (h p) -> (l d h) p", p=128)),
        ("hn", hg_norm_w.rearrange("l (h p) -> (l h) p", p=128)),
        ("mn", ml_norm_w.rearrange("l (h p) -> (l h) p", p=128)),
        ("cv", cvec),
        ("n0", st_n),
    ]
    voff = {}
    r = 0
    for name, ap in vec_src:
        voff[name] = r
        r += ap.shape[0]
    RT = r
    NCH = (RT + 127) // 128
    VST = sb("VST", [128, NCH, 128])
    V = sb("V", [128, NCH * 128])
    bVST = bufs["VST"]
    bV = bufs["V"]
    for ch_ in range(NCH):
        OP("pool", "tensor_copy", [bCST], [bVST], out=VST[:, ch_, :], in_=ZR[:, 0:128])
    for name, ap in vec_src:
        r0 = voff[name]
        n = ap.shape[0]
        done = 0
        while done < n:
            row = r0 + done
            ch, pr = row // 128, row % 128
            cnt = min(n - done, 128 - pr)
            DMA("sp", "ld_vst", [], [bVST], VST[pr:pr + cnt, ch, :], ap[done:done + cnt, :])
            done += cnt
    for ch in range(NCH):
        OP("pe", "transpose", [bVST, bCST], [bPA], PA[:, 0:128], VST[:, ch, :], ident)
        OP("act", "copy", [bPA], [bV], out=V[:, ch * 128:(ch + 1) * 128], in_=PA[:, 0:128])

    def vcol(name, idx):
        c = voff[name] + idx
        return V[:, c:c + 1]

    MB = sb("MB", [128, 32])
    DMA("sp", "ld_mb", [], [bufs["MB"]], MB[:], bass.AP(st_m.tensor, st_m.offset, [[0, 128], [1, 32]]))
    GB = sb("GB", [16, 4])
    S.dma("sp", lambda e: e.dma_start(out=GB[:], in_=ml_gate_b.rearrange("l g -> g l"), allow_slow_non_contiguous=True), "ld_gb", [], [bufs["GB"]])

    LBE = sb("LBE", [128, 32])
    LB = sb("LB", [128, 32])
    OML = sb("OML", [128, 32])
    LBT = sb("LBT", [128, 16])
    bLB = bufs["LB"]
    lo = voff["lb"]
    OP("act", "activation", [bV], [bufs["LBE"]], out=LBE[:], in_=V[:, lo:lo + 32], func=AF.Exp)
    OP("dve", "tensor_tensor", [bufs["LBE"]], [bufs["LBT"]], out=LBT[:, 0:8], in0=LBE[:, 0:8], in1=LBE[:, 8:16], op=ALU.add)
    OP("dve", "tensor_tensor", [bufs["LBE"], bufs["LBT"]], [bufs["LBT"]], out=LBT[:, 0:8], in0=LBT[:, 0:8], in1=LBE[:, 16:24], op=ALU.add)
    OP("dve", "tensor_tensor", [bufs["LBE"], bufs["LBT"]], [bufs["LBT"]], out=LBT[:, 0:8], in0=LBT[:, 0:8], in1=LBE[:, 24:32], op=ALU.add)
    OP("dve", "reciprocal", [bufs["LBT"]], [bufs["LBT"]], out=LBT[:, 8:16], in_=LBT[:, 0:8])
    OP("dve", "tensor_copy", [bCST], [bLB], out=LB[:, 0:8], in_=ZR[:, 0:8])
    OP("dve", "tensor_tensor", [bufs["LBE"], bufs["LBT"]], [bLB], out=LB[:, 8:16], in0=LBE[:, 8:16], in1=LBT[:, 8:16], op=ALU.mult)
    for l in (2, 3):
        OP("dve", "tensor_tensor", [bufs["LBE"], bufs["LBT"]], [bufs["LBE"]], out=LBE[:, 0:8], in0=LBE[:, l * 8:l * 8 + 8], in1=LBT[:, 8:16], op=ALU.mult)
        OP("dve", "tensor_tensor", [bufs["LBE"], bLB], [bLB], out=LB[:, l * 8:l * 8 + 8], in0=LBE[:, 0:8], in1=LB[:, (l - 1) * 8:l * 8], op=ALU.add)
    OP("dve", "tensor_scalar", [bLB], [bufs["OML"]], out=OML[:], in0=LB[:], scalar1=-1.0, scalar2=1.0, op0=ALU.mult, op1=ALU.add)
    bOML = bufs["OML"]

    NSLOT = 3
    WS = [sb("WS%d" % i, [128, 4096], BF16) for i in range(NSLOT)]
    WSB = [bufs["WS%d" % i] for i in range(NSLOT)]

    def rows(w2d):
        return w2d.rearrange("(c p) n -> p c n", p=128)

    witems = []

    def item_blk(w2d, c0, ncol, kch=8):
        def mk(slot):
            v = slot[:, 0:kch * ncol].rearrange("p (c n) -> p c n", c=kch)
            return [(v, rows(w2d)[:, :, c0:c0 + ncol])]
        return mk

    def item_heads(w2d, cols):
        def mk(slot):
            v = slot[:, 0:8 * len(cols) * 128].rearrange("p (c b n) -> p c b n", c=8, b=len(cols))
            return [(v[:, :, i, :], rows(w2d)[:, :, c0:c0 + 128]) for i, c0 in enumerate(cols)]
        return mk

    def item_wd(w2d, dc):
        def mk(slot):
            v = slot[:, 0:NFF * 128].rearrange("p (f n) -> p f n", f=NFF)
            return [(v, w2d.rearrange("(f p) n -> p f n", p=128)[:, :, dc * 128:(dc + 1) * 128])]
        return mk

    for l in range(nlayers):
        for pc in range(12):
            witems.append(item_blk(w_mod[l], pc * 512, 512))
    for job in jobs:
        for l in range(nlayers):
            if do_hg:
                witems.append(item_blk(w_in[l], 3 * 512, 512))
                for h in range(4):
                    witems.append(item_heads(w_in[l], [0 * 512 + h * 128, 1 * 512 + h * 128, 2 * 512 + h * 128, 4 * 512 + h * 128]))
            if do_ml:
                witems.append(item_blk(w_in[l], 7 * 512, 512))
                for h in range(4):
                    witems.append(item_heads(w_in[l], [5 * 512 + h * 128, 6 * 512 + h * 128, 8 * 512 + h * 128]))
            if do_hg or do_ml:
                witems.append(item_blk(w_out[l], 0, 512))
                witems.append(item_blk(w_out[l], 512, 512))
            if do_ffn:
                for tg in range(job.L // 1024):
                    for b in range(6):
                        ncol = 512 if b < 5 else 256
                        witems.append(item_blk(w_gate[l], b * 512, ncol))
                        witems.append(item_blk(w_up[l], b * 512, ncol))
                    for dc in range(8):
                        witems.append(item_wd(w_down[l], dc))
    wstate = {"cur": 0, "issued": 0}

    def w_issue(k):
        slot = k % NSLOT
        pairs = witems[k](WS[slot])

        def fn(e, pairs=pairs):
            return [e.dma_start(out=o, in_=i) for (o, i) in pairs]
        S.dma("pool", fn, "w%d" % slot, [], [WSB[slot]], n=len(pairs))

    def w_take():
        k = wstate["cur"]
        while wstate["issued"] < min(len(witems), k + NSLOT - 1):
            w_issue(wstate["issued"])
            wstate["issued"] += 1
        if wstate["issued"] <= k:
            w_issue(k)
            wstate["issued"] = k + 1
        wstate["cur"] = k + 1
        return WS[k % NSLOT], WSB[k % NSLOT]

    CSF = sb("CSF", [128, 8, 2])
    CSB = sb("CSB", [128, 8, 2], BF16)
    cvo = voff["cv"]
    OP("dve", "tensor_copy", [bV], [bufs["CSF"]], out=CSF[:, :, 0], in_=V[:, cvo:cvo + 8])
    OP("dve", "tensor_copy", [bV], [bufs["CSF"]], out=CSF[:, :, 1], in_=V[:, cvo + 8:cvo + 16])
    OP("act", "activation", [bufs["CSF"]], [bufs["CSB"]], out=CSB[:], in_=CSF[:], func=AF.Silu)
    MODS = sb("MODS", [128, 4, 48, 2])
    bMODS = bufs["MODS"]
    for l in range(nlayers):
        for pc in range(12):
            slot, sbuf_ = w_take()
            wv = slot[:, 0:4096].rearrange("p (c n) -> p c n", c=8)
            for m in range(4):
                blk = pc * 4 + m
                for c in range(8):
                    OP("pe", "matmul", [sbuf_, bufs["CSB"]], [bPMISC], PMISC[:, blk * 2:blk * 2 + 2],
                       lhsT=wv[:, c, m * 128:(m + 1) * 128], rhs=CSB[:, c, :], start=(c == 0), stop=(c == 7))
        bo = voff["bm"] + l * 48
        OP("dve", "tensor_tensor", [bPMISC, bV], [bMODS], out=MODS[:, l, :, :],
           in0=PMISC[:, 0:96].rearrange("p (m j) -> p m j", j=2),
           in1=V3(V[:, bo:bo + 48], [[1, 48], [0, 2]]), op=ALU.add)
    SC = sb("SC", [128, 4, 2, 8, 2])
    bSC = bufs["SC"]
    for l in range(nlayers):
        for wh, (mo, nn) in enumerate(((8, "n1"), (32, "n2"))):
            no = voff[nn] + l * 8
            OP("dve", "scalar_tensor_tensor", [bMODS, bV], [bSC], out=SC[:, l, wh, :, :], in0=MODS[:, l, mo:mo + 8, :],
               scalar=1.0, in1=V3(V[:, no:no + 8], [[1, 8], [0, 2]]), op0=ALU.add, op1=ALU.mult)

    def modc(l, which, c, j):
        return MODS[:, l, which * 8 + c, j:j + 1]

    REGA = sb("REGA", [128, 4096])
    REGB = sb("REGB", [128, 4096])
    REGC = sb("REGC", [128, 2048])
    XG = [REGA[:, :].rearrange("p (c n) -> p c n", c=8), REGB[:, :].rearrange("p (c n) -> p c n", c=8)]
    bXG = [Buf("xg0"), Buf("xg1")]
    XTOK = [REGC[:, 0:1024], REGC[:, 1024:2048]]
    bXTOK = [Buf("xtok0"), Buf("xtok1")]
    SQK = sb("SQK", [128, 4096], BF16)
    SQ = SQK[:, :].rearrange("p (c n) -> p c n", c=8)
    bSQ = Buf("sq")
    RSTD = sb("RSTD", [128, 512])
    bRSTD = bufs["RSTD"]
    LNV = sb("LNV", [128, 512])
    bLNV = bufs["LNV"]
    LMAX = 2048
    HT = sb("HT", [128, 8, LMAX], BF16)
    bHT = [Buf("ht%d" % g) for g in range(LMAX // 512)]
    mixd = {"S": nc.dram_tensor("mixd_s", [128, 8, 2048], BF16).ap(), "P": nc.dram_tensor("mixd_p", [128, 8, 1024], BF16).ap()}
    bMIXD = {"S": [Buf("mixdS%d" % g) for g in range(4)], "P": [Buf("mixdP%d" % g) for g in range(2)]}
    VTOK = REGB[:, :].bitcast(BF16).rearrange("p (t n) -> p t n", n=512)
    bVTOK = [Buf("vtok%d" % g) for g in range(LMAX // 512)]
    ARENA = sb("ARENA", [128, 44 * 1024 // 4])
    def arena_f32(off_words, n):
        return ARENA[:, off_words:off_words + n]
    QF = arena_f32(0, 2048)
    OACC = arena_f32(2048, 2048)
    PRE = arena_f32(0, 2048)
    HACC = PRE
    ACC = arena_f32(2048, 2048)
    QKC = ARENA[:, 4096:6144].bitcast(BF16)
    QC = QKC[:, 0:2048]
    KCV = QKC[:, 2048:4096]
    G16 = ARENA[0:16, 6144:6144 + 2048]
    ABUF = ARENA[:, 0:11264].bitcast(BF16)
    MIXG = ARENA[:, 8192:10240].bitcast(BF16).rearrange("p (c n) -> p c n", c=8)
    bMIXG = Buf("mixg")
    SEL = ARENA[0:16, 8192:10240]
    bSEL = Buf("sel")
    bQF = [Buf("qf%d" % g) for g in range(4)]
    bOACC = [Buf("oacc%d" % g) for g in range(4)]
    bHACC = [Buf("hacc%d" % g) for g in range(4)]
    bPRE = bHACC
    bACC = Buf("acc")
    bQC = [Buf("qc%d" % g) for g in range(4)]
    bKCV = [Buf("kcv%d" % g) for g in range(4)]
    bG16 = [Buf("g16%d" % g) for g in range(4)]
    bABUF = [Buf("abuf%d" % s_) for s_ in range(2)]
    arena_all = bQF + bOACC + [bACC] + bHACC + bQC + bKCV + bG16 + bABUF + [bMIXG, bSEL]

    NT_ = 12
    TMP = [REGC[:, i * 512:(i + 1) * 512] for i in range(4)] + [REGA[:, i * 512:(i + 1) * 512] for i in range(8)]
    bTMP = [Buf("T%d" % i) for i in range(NT_)]
    TB = [sb("TB%d" % i, [128, 512], BF16) for i in range(6)]
    bTB = [bufs["TB%d" % i] for i in range(6)]
    KHAT = SQK[:, :].rearrange("p (i a b) -> p i a b", i=8, a=4)
    bKHAT = Buf("khat")
    FDUM = sb("FDUM", [128, 2])
    def get_alias():
        return bXG + bXTOK + [bSQ, bKHAT, bufs["SQK2"]] + bMLX + bVTOK + arena_all + bTMP

    def interleave(a, b, ratio=(1, 1)):
        da = db = False
        if b is None:
            db = True
        while not (da and db):
            for _ in range(ratio[1]):
                if db:
                    break
                try:
                    next(b)
                except StopIteration:
                    db = True
            for _ in range(ratio[0]):
                if da:
                    break
                try:
                    next(a)
                except StopIteration:
                    da = True

    def drain(a):
        for _ in a:
            pass

    def three_way(a, b, c):
        gens = [x for x in (a, b, c) if x is not None]
        while gens:
            for x in list(gens):
                try:
                    next(x)
                except StopIteration:
                    gens.remove(x)

    def fence_all():
        OP("dve", "tensor_copy", [bCST], get_alias() + [bufs["FDUM"]], out=FDUM[:, 0:1], in_=ZR[:, 0:1])

    SMALLH = [sb("SMALL0", [128, 64]), sb("SMALL1", [128, 64])]
    bSMALLH = [bufs["SMALL0"], bufs["SMALL1"]]
    TBH = [[TB[0], TB[1], TB[2]], [sb("TBX%d" % i, [128, 512], BF16) for i in range(3)]]
    bTBH = [[bTB[0], bTB[1], bTB[2]], [bufs["TBX%d" % i] for i in range(3)]]
    SQK2 = sb("SQK2", [128, 4096], BF16)
    KHATH = [KHAT, SQK2[:, :].rearrange("p (i a b) -> p i a b", i=8, a=4)]
    bKHATH = [bKHAT, bufs["SQK2"]]
    FQ = [sb("FQ%d" % i, [128, 512]) for i in range(3)]
    bFQ = [bufs["FQ%d" % i] for i in range(3)]
    EMH = [sb("EMH0", [128, 512]), sb("EMH1", [128, 512])]
    bEMH = [bufs["EMH0"], bufs["EMH1"]]
    MLX = [SQK2[:, :].bitcast(F32)[:, i * 512:(i + 1) * 512] for i in range(4)]
    bMLX = [Buf("mlx%d" % i) for i in range(4)]
    SST = sb("SST", [128, 128])
    SSTB = sb("SSTB", [128, 128], BF16)
    NREP = sb("NREP", [128, 128])
    NREPB = sb("NREPB", [128, 128], BF16)
    MS = sb("MS", [128, 96])
    bSST, bSSTB, bNREP, bNREPB, bMS = bufs["SST"], bufs["SSTB"], bufs["NREP"], bufs["NREPB"], bufs["MS"]
    YTOK = XTOK
    bYTOK = bXTOK
    WG = sb("WG", [128, 8, 16], BF16)
    bWG = bufs["WG"]
    out_bufs = []

    def rms_rstd(src_sq_ap_list, bsrc, inv_n, psum, bpsum):
        n = len(src_sq_ap_list)
        for i, a in enumerate(src_sq_ap_list):
            OP("pe", "matmul", [bONES] + bsrc, [bpsum], psum[:, :], lhsT=ONES[:], rhs=a, start=(i == 0), stop=(i == n - 1))
        OP("act", "activation", [bpsum, bKC], [bLNV], out=LNV[:], in_=psum[:, :], func=AF.Ln, bias=EPSC, scale=inv_n)
        OP("act", "activation", [bLNV], [bRSTD], out=RSTD[:], in_=LNV[:], func=AF.Exp, scale=-0.5)

    def norm_to_ht(job, l, wh, g, xg, bxg):
        j = job.cond
        gc = slice(g * 512, (g + 1) * 512)
        OP("act", "activation", [bxg], [bSQ], out=SQ[:].rearrange("p c n -> p (c n)"),
           in_=xg[:].rearrange("p c n -> p (c n)"), func=AF.Square)
        rms_rstd([SQ[:, c, :] for c in range(8)], [bSQ], 1.0 / D, PA, bPA)
        for c in range(8):
            OP("dve", "scalar_tensor_tensor", [bxg, bSC, bRSTD], [bxg], out=xg[:, c, :], in0=xg[:, c, :],
               scalar=SC[:, l, wh, c, j:j + 1], in1=RSTD[:], op0=ALU.mult, op1=ALU.mult)
            OP("act", "activation", [bxg, bMODS], [bHT[g]], out=HT[:, c, gc], in_=xg[:, c, :],
               func=AF.Identity, bias=modc(l, 0 if wh == 0 else 3, c, j), scale=1.0)

    pp = {"i": 0}

    def inproj(job, g, wv_fn, wbuf, ncols=128):
        i = pp["i"]
        pp["i"] ^= 1
        ps, bps = (PA, bPA) if i == 0 else (PB_, bPB)
        gc = slice(g * 512, (g + 1) * 512)
        for c in range(8):
            OP("pe", "matmul", [wbuf, bHT[g]], [bps], ps[0:ncols, :], lhsT=wv_fn(c), rhs=HT[:, c, gc],
               start=(c == 0), stop=(c == 7))
        return ps, bps

    def make_vtok(job, slot, sbuf_):
        wv = slot[:, 0:4096].rearrange("p (c n) -> p c n", c=8)
        for t in range(job.NT):
            i = pp["i"]
            pp["i"] ^= 1
            ps, bps = (PA, bPA) if i == 0 else (PB_, bPB)
            g = t // 4
            for c in range(8):
                OP("pe", "matmul", [sbuf_, bHT[g]], [bps], ps[:, :], lhsT=HT[:, c, t * 128:(t + 1) * 128], rhs=wv[:, c, :],
                   start=(c == 0), stop=(c == 7))
            if t % 2 == 0:
                OP("act", "copy", [bps], [bVTOK[g]], out=VTOK[:, t, :], in_=ps[:, :])
            else:
                OP("dve", "tensor_copy", [bps], [bVTOK[g]], out=VTOK[:, t, :], in_=ps[:, :])

    def head_finalize(job, l, g, gate_ps, bgate_ps, gate_func, acc_ap, bacc, normw_col, mix_head, temps=None):
        gc = slice(g * 512, (g + 1) * 512)
        if temps is None:
            temps = ((TMP[0], bTMP[0]), (TMP[2], bTMP[2]), (TMP[1], bTMP[1]))
        (tg, btg), (tsc, btsc), (t1, bt1) = temps
        sigm(gate_ps[:, :], [bgate_ps], tg[:], btg, tsc[:], btsc)
        if gate_func == AF.Silu:
            OP("dve", "tensor_tensor", [bgate_ps, btg], [btg], out=tg[:], in0=gate_ps[:, :], in1=tg[:], op=ALU.mult)
        yield
        OP("act", "activation", [bacc], [bTB[3]], out=TB[3][:], in_=acc_ap, func=AF.Square)
        rms_rstd([TB[3][:]], [bTB[3]], 1.0 / 128, PMISC, bPMISC)
        yield
        OP("dve", "scalar_tensor_tensor", [bacc, bV, bRSTD], [bt1], out=t1[:], in0=acc_ap, scalar=normw_col,
           in1=RSTD[:], op0=ALU.mult, op1=ALU.mult)
        OP("dve", "tensor_tensor", [bt1, btg], [bTB[5]], out=TB[5][:], in0=t1[:], in1=tg[:], op=ALU.mult)
        DMA("sp", "st_mix%d" % g, [bTB[5]], [bMIXD[job.name][g]], mixd[job.name][:, mix_head, gc], TB[5][:])
        yield

    def hg_head(job, l, h, slot, sbuf_):
        wv = slot[:, 0:4096].rearrange("p (c b n) -> p c b n", c=8, b=4)
        G = job.G
        SUB = 32
        NSB = 128 // SUB
        NR = 512 // SUB

        def qpass():
            for g in range(G):
                ps, bps = inproj(job, g, lambda c: wv[:, c, 0, :], sbuf_)
                sigm(ps[:, :], [bps], FQ[1][:], bFQ[1], FQ[2][:], bFQ[2])
                OP("dve", "tensor_tensor", [bps, bFQ[1]], [bQF[g]], out=QF[:, g * 512:(g + 1) * 512], in0=ps[:, :], in1=FQ[1][:], op=ALU.mult)
                yield

        units = [(0, g) for g in range(G)] + [(1, g) for g in range(G - 1, -1, -1)]
        pend = {}

        def issue_inproj(u):
            d_, g_ = units[u]
            pend[u] = inproj(job, g_, lambda c: wv[:, c, 1 + d_, :], sbuf_)

        def prep(d, g, hs, u):
            fwd = (d == 0)
            lbi = (l * 2 + d) * 4 + h
            gc0 = g * 512
            Qs, Qt, Kend = TBH[hs]
            bQs, bQt, bKend = bTBH[hs]
            KH, bKH = KHATH[hs], bKHATH[hs]
            SM, bSM = SMALLH[hs], bSMALLH[hs]
            if u not in pend:
                issue_inproj(u)
            ps, bps = pend.pop(u)
            t_e, t_lf, t_l2, t_k, t_B = TMP[0], TMP[1], TMP[2], TMP[3], TMP[4]
            b_lf = bTMP[1]
            OP("act", "activation", [bps], [bTMP[0]], out=t_e[:], in_=ps[:, :], func=AF.Exp, scale=-1.0)
            OP("act", "activation", [bTMP[0], bLB, bCST], [bTMP[1]], out=t_lf[:], in_=t_e[:], func=AF.Ln, bias=ONE, scale=LB[:, lbi:lbi + 1])
            OP("act", "activation", [bTMP[0], bCST], [bTMP[2]], out=t_l2[:], in_=t_e[:], func=AF.Ln, bias=ONE, scale=1.0)
            OP("act", "activation", [bTMP[2]], [bTMP[3]], out=t_k[:], in_=t_l2[:], func=AF.Exp, scale=-1.0)
            if u >= 1 and u + 1 < len(units):
                issue_inproj(u + 1)
            yield
            OP("dve", "tensor_tensor", [bTMP[1], bTMP[2]], [bTMP[1]], out=t_lf[:], in0=t_lf[:], in1=t_l2[:], op=ALU.subtract)
            OP("dve", "scalar_tensor_tensor", [bTMP[0], bOML, bTMP[3]], [bTMP[3]], out=t_k[:], in0=t_e[:], scalar=OML[:, lbi:lbi + 1],
               in1=t_k[:], op0=ALU.mult, op1=ALU.mult)
            if fwd:
                OP("dve", "tensor_tensor_scan", [bTMP[1], bCST], [bTMP[4]], out=t_B[:], data0=CST[:, C_RMF:C_RMF + 512],
                   data1=t_lf[:], initial=0.0, op0=ALU.mult, op1=ALU.add)
            else:
                OP("dve", "tensor_tensor_scan", [bTMP[1], bCST], [bTMP[4]], out=V3(t_B[:, 511:512], [[-1, 512]]),
                   data0=V3(CST[:, C_RMB + 511:C_RMB + 512], [[-1, 512]]),
                   data1=V3(t_lf[:, 511:512], [[-1, 512]]), initial=0.0, op0=ALU.mult, op1=ALU.add)
            e0 = 0 if fwd else SUB - 1
            el = 127 if fwd else 0
            R = SM[:, 0:NR]
            ATL = SM[:, 32:36]
            B3 = t_B[:].rearrange("p (a b) -> p a b", b=128)
            OP("dve", "tensor_tensor", [bTMP[4], bTMP[1]], [bSM], out=R, in0=V3(t_B[:, e0:e0 + 1], [[SUB, NR]]),
               in1=V3(t_lf[:, e0:e0 + 1], [[SUB, NR]]), op=ALU.subtract)
            OP("dve", "tensor_tensor", [bTMP[4], bSM], [bTMP[5]], out=TMP[5][:].rearrange("p (a b) -> p a b", b=SUB),
               in0=t_B[:].rearrange("p (a b) -> p a b", b=SUB), in1=V3(R[:, 0:1], [[1, NR], [0, SUB]]), op=ALU.subtract)
            OP("dve", "tensor_tensor", [bTMP[4]], [bTMP[0]], out=TMP[0][:].rearrange("p (a b) -> p a b", b=128),
               in0=V3(t_B[:, el:el + 1], [[128, 4], [0, 128]]), in1=B3, op=ALU.subtract)
            yield
            OP("act", "activation", [bTMP[5]], [bTMP[6]], out=TMP[6][:], in_=TMP[5][:], func=AF.Exp)
            OP("act", "activation", [bTMP[4]], [bTMP[7]], out=TMP[7][:], in_=t_B[:], func=AF.Exp)
            OP("act", "activation", [bTMP[4]], [bSM], out=ATL, in_=V3(t_B[:, el:el + 1], [[128, 4]]), func=AF.Exp)
            OP("act", "activation", [bTMP[0]], [bTMP[2]], out=TMP[2][:], in_=TMP[0][:], func=AF.Exp)

            def kh_sub(i):
                w = SUB * (i + 1) if fwd else 128
                ta, bta = TMP[8 + (i % 2) * 2], bTMP[8 + (i % 2) * 2]
                tav = ta[:].rearrange("p (a b) -> p a b", b=128)[:, :, 0:w]
                OP("dve", "tensor_tensor", [bTMP[4], bSM], [bta], out=tav, in0=V3(R[:, i:i + 1], [[NSB, 4], [0, w]]),
                   in1=B3[:, :, 0:w], op=ALU.subtract)

            def kh_act(i):
                w = SUB * (i + 1) if fwd else 128
                ta, te = TMP[8 + (i % 2) * 2], TMP[9 + (i % 2) * 2]
                bta, bte = bTMP[8 + (i % 2) * 2], bTMP[9 + (i % 2) * 2]
                tav = ta[:].rearrange("p (a b) -> p a b", b=128)[:, :, 0:w]
                tev = te[:].rearrange("p (a b) -> p a b", b=128)[:, :, 0:w]
                if not fwd:
                    OP("act", "activation", [bta, bCST], [bta], out=tav, in_=tav, func=AF.Relu, bias=C60, scale=-1.0)
                    OP("act", "activation", [bta, bCST], [bte], out=tev, in_=tav, func=AF.Exp, bias=C60, scale=-1.0)
                else:
                    OP("act", "activation", [bta], [bte], out=tev, in_=tav, func=AF.Exp)

            def kh_mul(i):
                w = SUB * (i + 1) if fwd else 128
                te, bte = TMP[9 + (i % 2) * 2], bTMP[9 + (i % 2) * 2]
                tev = te[:].rearrange("p (a b) -> p a b", b=128)[:, :, 0:w]
                OP("dve", "tensor_tensor", [bte, bTMP[3]], [bKH], out=KH[:, i, :, 0:w], in0=tev,
                   in1=t_k[:].rearrange("p (a b) -> p a b", b=128)[:, :, 0:w], op=ALU.mult)

            kh_sub(0)
            kh_sub(1)
            yield
            kh_act(0)
            kh_act(1)
            OP("dve", "tensor_tensor", [bTMP[6], bQF[g]], [bQs], out=Qs[:], in0=TMP[6][:], in1=QF[:, gc0:gc0 + 512], op=ALU.mult)
            OP("pool", "tensor_tensor", [bTMP[7], bQF[g]], [bQt], out=Qt[:], in0=TMP[7][:], in1=QF[:, gc0:gc0 + 512], op=ALU.mult)
            yield
            OP("dve", "tensor_tensor", [bTMP[2], bTMP[3]], [bKend], out=Kend[:], in0=TMP[2][:], in1=t_k[:], op=ALU.mult)
            for i in range(NSB):
                kh_mul(i)
                if i + 2 < NSB:
                    kh_sub(i + 2)
                    kh_act(i + 2)
                yield

        def main(d, g, hs):
            fwd = (d == 0)
            gc0 = g * 512
            Qs, Qt, Kend = TBH[hs]
            bQs, bQt, bKend = bTBH[hs]
            KH, bKH = KHATH[hs], bKHATH[hs]
            SM, bSM = SMALLH[hs], bSMALLH[hs]
            ATL = SM[:, 32:36]
            mask_bf = CB[:, 128:256] if fwd else CB[:, 256:384]
            for n in range(4):
                for i in range(NSB):
                    w = SUB * (i + 1) if fwd else 128
                    c0 = n * 128 + SUB * i
                    OP("pe", "matmul", [bKH, bQs], [bPAT], PAT[0:w, c0:c0 + SUB], lhsT=KH[:, i, n, 0:w],
                       rhs=Qs[:, c0:c0 + SUB], start=True, stop=True)
            for n in range(4):
                OP("pe", "transpose", [bKend, bCB], [PKTB], PKT[:, n * 128:(n + 1) * 128], Kend[:, n * 128:(n + 1) * 128], ident_bf)
            yield
            OP("dve", "tensor_tensor", [bPAT, bCB], [bTB[3]], out=TB[3][:].rearrange("p (a b) -> p a b", b=128),
               in0=PAT[:, :].rearrange("p (a b) -> p a b", b=128), in1=V3(mask_bf[:, 0:1], [[0, 4], [1, 128]]), op=ALU.mult)
            ATs = TB[3]
            OP("act", "copy", [PKTB], [bTB[4]], out=TB[4][:], in_=PKT[:, :])
            KT = TB[4]
            for n in range(4):
                t = g * 4 + n
                OP("pe", "matmul", [bTB[4], bVTOK[g]], [bPKV], PKV[:, n * 128:(n + 1) * 128], lhsT=KT[:, n * 128:(n + 1) * 128],
                   rhs=VTOK[:, t, h * 128:(h + 1) * 128], start=True, stop=True)
            yield
            norder = range(4) if fwd else range(3, -1, -1)
            for n in norder:
                t = g * 4 + n
                sq_, ts = t // job.tps, t % job.tps
                first = (ts == 0) if fwd else (ts == job.tps - 1)
                last = (ts == job.tps - 1) if fwd else (ts == 0)
                if first:
                    if job.init_cache:
                        DMA("sp", "ld_sst", [], [bSST], SST[:], st_hg[l, d, h, :, :])
                        OP("act", "copy", [bSST], [bSSTB], out=SSTB[:], in_=SST[:])
                    else:
                        OP("pool", "tensor_copy", [bCST], [bSST], out=SST[:], in_=ZR[:, 0:128])
                        OP("pool", "tensor_copy", [bCST], [bSSTB], out=SSTB[:], in_=ZR[:, 0:128])
                OP("pe", "matmul", [bTB[3], bVTOK[g]], [bPO], PO[:, n * 128:(n + 1) * 128], lhsT=VTOK[:, t, h * 128:(h + 1) * 128],
                   rhs=ATs[:, n * 128:(n + 1) * 128], start=True, stop=False)
                OP("pe", "matmul", [bSSTB, bQt], [bPO], PO[:, n * 128:(n + 1) * 128], lhsT=SSTB[:],
                   rhs=Qt[:, n * 128:(n + 1) * 128], start=False, stop=True)
                OP("dve", "scalar_tensor_tensor", [bSST, bSM, bPKV], [bSST], out=SST[:], in0=SST[:], scalar=ATL[:, n:n + 1],
                   in1=PKV[:, n * 128:(n + 1) * 128], op0=ALU.mult, op1=ALU.add)
                if last and job.state_out:
                    DMA("sp", "st_sst", [bSST], [], ns_hg[sq_, l, d, h, :, :], SST[:])
                if not last:
                    OP("act", "copy", [bSST], [bSSTB], out=SSTB[:], in_=SST[:])
                yield
            if fwd:
                OP("act", "copy", [bPO], [bOACC[g]], out=OACC[:, gc0:gc0 + 512], in_=PO[:, :])
            else:
                OP("dve", "tensor_tensor", [bPO, bOACC[g]], [bOACC[g]], out=OACC[:, gc0:gc0 + 512], in0=PO[:, :],
                   in1=OACC[:, gc0:gc0 + 512], op=ALU.add)
            yield

        def prep0():
            return prep(units[0][0], units[0][1], 0, 0)

        def run_units():
            for u, (d, g) in enumerate(units):
                nxt = prep(units[u + 1][0], units[u + 1][1], (u + 1) % 2, u + 1) if u + 1 < len(units) else None
                interleave(main(d, g, u % 2), nxt, ratio=(1, 1))

        def fin():
            ftemps = ((EMH[0], bEMH[0]), (EMH[1], bEMH[1]), (FQ[0], bFQ[0]))
            for g in range(G):
                ps, bps = inproj(job, g, lambda c: wv[:, c, 3, :], sbuf_)
                for _ in head_finalize(job, l, g, ps, bps, AF.Silu, OACC[:, g * 512:(g + 1) * 512], bOACC[g], vcol("hn", l * 4 + h), h,
                                       temps=ftemps):
                    yield

        return qpass, prep0, run_units, fin

    def conv_qk(job, l, ch, src, bsrc, dst_acc, eng):
        L = job.L
        def wc(a, b):
            return vcol("cw", ((l * 3 + a) * 3 + b) * 8 + ch)
        OP(eng, "tensor_scalar", bsrc + [bV], [bACC], out=dst_acc[:, 0:L], in0=src[:, 0:L], scalar1=wc(1, 1),
           scalar2=vcol("cb", l * 8 + ch), op0=ALU.mult, op1=ALU.add)
        yield
        if job.conv2d:
            Wd = 64
            R_ = L // Wd
            sv = src[:, 0:L].rearrange("p (r c) -> p r c", c=Wd)
            dv = dst_acc[:, 0:L].rearrange("p (r c) -> p r c", c=Wd)
            taps = [(a, b) for a in range(3) for b in range(3) if not (a == 1 and b == 1)]
            for (a, b) in taps:
                dr, dc = a - 1, b - 1
                r0, r1 = max(0, -dr), R_ - max(0, dr)
                c0, c1 = max(0, -dc), Wd - max(0, dc)
                OP(eng, "scalar_tensor_tensor", bsrc + [bV, bACC], [bACC], out=dv[:, r0:r1, c0:c1],
                   in0=sv[:, r0 + dr:r1 + dr, c0 + dc:c1 + dc], scalar=wc(a, b), in1=dv[:, r0:r1, c0:c1],
                   op0=ALU.mult, op1=ALU.add)
                yield
        else:
            Wd = job.seqlen
            sv = src[:, 0:L].rearrange("p (r c) -> p r c", c=Wd)
            dv = dst_acc[:, 0:L].rearrange("p (r c) -> p r c", c=Wd)
            for b in (0, 2):
                dc = b - 1
                c0, c1 = max(0, -dc), Wd - max(0, dc)
                OP(eng, "scalar_tensor_tensor", bsrc + [bV, bACC], [bACC], out=dv[:, :, c0:c1],
                   in0=sv[:, :, c0 + dc:c1 + dc], scalar=wc(1, b), in1=dv[:, :, c0:c1], op0=ALU.mult, op1=ALU.add)
                yield

    def ml_gates(job, l):
        DMA("sp", "ld_sel", [], [bSEL], SEL, selc[:, :])
        S.dma("pool", lambda e: e.dma_start(out=WG[:], in_=rows(w_in[l])[:, :, 4608:4624]), "ld_wg", [], [bWG])
        for g in range(job.G):
            ps, bps = inproj(job, g, lambda c: WG[:, c, :], bWG, ncols=16)
            OP("act", "activation", [bps, bufs["GB"]], [bG16[g]], out=G16[:, g * 512:(g + 1) * 512], in_=ps[0:16, :],
               func=AF.Identity, bias=GB[:, l:l + 1], scale=1.0)

    def ml_head(job, l, h, slot, sbuf_):
        wv = slot[:, 0:3072].rearrange("p (c b n) -> p c b n", c=8, b=3)
        G = job.G
        L = job.L
        def qpass():
            for qi in range(2):
                for g in range(G):
                    ps, bps = inproj(job, g, lambda c: wv[:, c, qi, :], sbuf_)
                    OP("act", "copy", [bps], [bPRE[g]], out=PRE[:, g * 512:(g + 1) * 512], in_=ps[:, :])
                    yield
                for _ in conv_qk(job, l, qi * 4 + h, PRE, bPRE[0:G], ACC, "dve"):
                    yield
                for g in range(G):
                    gs = slice(g * 512, (g + 1) * 512)
                    sigm(ACC[:, gs], [bACC], TMP[6][:], bTMP[6], TMP[7][:], bTMP[7])
                    if qi == 0:
                        OP("dve", "tensor_tensor", [bACC, bTMP[6]], [bQC[g]], out=QC[:, gs], in0=ACC[:, gs], in1=TMP[6][:], op=ALU.mult)
                    else:
                        OP("dve", "scalar_tensor_tensor", [bACC, bTMP[6]], [bKCV[g]], out=KCV[:, gs], in0=ACC[:, gs],
                           scalar=float(128 ** -0.5), in1=TMP[6][:], op0=ALU.mult, op1=ALU.mult)
                    yield

        msl = {"i": 0}

        def ms_next():
            msl["i"] = (msl["i"] + 1) % 96
            return MS[:, msl["i"]:msl["i"] + 1]

        HSETS = [([TMP[1], TMP[3], TMP[4], TMP[8]], [bTMP[1], bTMP[3], bTMP[4], bTMP[8]]), (MLX, bMLX)]
        chain = {"mstart": None}

        units = [(0, g) for g in range(G)] + [(1, g) for g in range(G - 1, -1, -1)]
        issued = set()

        def issue_bc(u):
            d_, g_ = units[u]
            ri_ = d_ * 4 + h
            rf_ = 8 + d_ * 4 + h
            c_ = g_ * 512
            OP("pe", "matmul", [bSEL, bG16[g_]], [bPA], PA[:, :], lhsT=SEL[:, ri_ * 128:(ri_ + 1) * 128], rhs=G16[:, c_:c_ + 512], start=True, stop=True)
            OP("pe", "matmul", [bSEL, bG16[g_]], [bPB], PB_[:, :], lhsT=SEL[:, rf_ * 128:(rf_ + 1) * 128], rhs=G16[:, c_:c_ + 512], start=True, stop=True)
            issued.add(u)

        def prep(d, g, hs, u):
            fwd = (d == 0)
            sidx = (l * 2 + d) * 4 + h
            gc0 = g * 512
            (t_b, t_M, t_ai, t_PTm), (b_b, b_M, b_ai, b_PTm) = HSETS[hs]
            SM, bSM = SMALLH[hs], bSMALLH[hs]
            mask_f = CST[:, C_MASKU:C_MASKU + 128] if fwd else CST[:, C_MASKL:C_MASKL + 128]
            if u not in issued:
                issue_bc(u)
            yield
            t_lf, t_w, t_PT = TMP[0], TMP[2], TMP[5]
            OP("act", "activation", [bPB], [bTMP[6]], out=TMP[6][:], in_=PB_[:, :], func=AF.Exp, scale=-1.0)
            OP("act", "activation", [bTMP[6], bCST], [bTMP[0]], out=t_lf[:], in_=TMP[6][:], func=AF.Ln, bias=ONE, scale=1.0)
            if fwd:
                OP("dve", "tensor_tensor_scan", [bTMP[0], bCST], [b_b], out=t_b[:], data0=CST[:, C_RMF:C_RMF + 512],
                   data1=t_lf[:], initial=0.0, op0=ALU.mult, op1=ALU.subtract)
            else:
                OP("dve", "tensor_tensor_scan", [bTMP[0], bCST], [b_b], out=V3(t_b[:, 511:512], [[-1, 512]]),
                   data0=V3(CST[:, C_RMB + 511:C_RMB + 512], [[-1, 512]]),
                   data1=V3(t_lf[:, 511:512], [[-1, 512]]), initial=0.0, op0=ALU.mult, op1=ALU.subtract)
            OP("dve", "tensor_tensor", [bPA, b_b], [bTMP[2]], out=t_w[:], in0=PA[:, :], in1=t_b[:], op=ALU.subtract)
            if u + 1 < len(units):
                issue_bc(u + 1)
            yield
            OP("dve", "tensor_tensor", [bTMP[2], bCST], [bTMP[7]], out=TMP[7][:].rearrange("p (a b) -> p a b", b=128),
               in0=t_w[:].rearrange("p (a b) -> p a b", b=128), in1=V3(ident[:, 0:1], [[0, 4], [1, 128]]), op=ALU.mult)
            WCOL = SM[:, 40:44]
            PEC = SM[:, 44:48]
            OP("dve", "tensor_reduce", [bTMP[7]], [bSM], out=WCOL, in_=TMP[7][:].rearrange("p (a b) -> p a b", b=128),
               axis=AX.X, op=ALU.add)
            yield
            norder = list(range(4)) if fwd else list(range(3, -1, -1))
            for n in norder:
                t = g * 4 + n
                sq_, ts = t // job.tps, t % job.tps
                first = (ts == 0) if fwd else (ts == job.tps - 1)
                last = (ts == job.tps - 1) if fwd else (ts == 0)
                if first:
                    chain["mstart"] = MB[:, sidx:sidx + 1] if job.init_cache else ZERO
                mstart = chain["mstart"]
                cs = n * 128
                if fwd:
                    OP("dve", "tensor_tensor_scan", [bTMP[2], bCST, bMS, bufs["MB"]], [b_M], out=t_M[:, cs:cs + 128],
                       data0=CST[:, C_NEGBIG:C_NEGBIG + 128], data1=t_w[:, cs:cs + 128], initial=mstart, op0=ALU.max, op1=ALU.max)
                    lc = cs + 127
                else:
                    OP("dve", "tensor_tensor_scan", [bTMP[2], bCST, bMS, bufs["MB"]], [b_M], out=V3(t_M[:, cs + 127:cs + 128], [[-1, 128]]),
                       data0=CST[:, C_NEGBIG:C_NEGBIG + 128], data1=V3(t_w[:, cs + 127:cs + 128], [[-1, 128]]),
                       initial=mstart, op0=ALU.max, op1=ALU.max)
                    lc = cs
                mnext = ms_next()
                OP("dve", "tensor_tensor", [b_b, b_M], [bMS], out=mnext, in0=t_b[:, lc:lc + 1], in1=t_M[:, lc:lc + 1], op=ALU.add)
                OP("act", "activation", [b_M, bMS, bufs["MB"], bCST], [b_ai], out=t_ai[:, cs:cs + 128], in_=t_M[:, cs:cs + 128],
                   func=AF.Exp, bias=mstart, scale=-1.0)
                OP("act", "activation", [b_M, bSM], [bTMP[5]], out=t_PT[:, cs:cs + 128], in_=t_M[:, cs:cs + 128],
                   func=AF.Exp, bias=WCOL[:, n:n + 1], scale=-1.0)
                if last and job.state_out:
                    DMA("sp", "st_m", [bMS], [], ns_m[sq_, l, d:d + 1, h:h + 1], mnext[0:1, :])
                chain["mstart"] = mnext
                yield
            OP("dve", "tensor_tensor", [b_b, b_M], [bTMP[7]], out=TMP[7][:], in0=t_b[:], in1=t_M[:], op=ALU.add)
            OP("act", "activation", [bTMP[7]], [bEMH[hs]], out=EMH[hs][:], in_=TMP[7][:], func=AF.Exp, scale=-1.0)
            yield
            lco = 127 if fwd else 0
            OP("act", "copy", [bTMP[5]], [bSM], out=PEC, in_=V3(t_PT[:, lco:lco + 1], [[128, 4]]))
            OP("pool", "tensor_tensor", [bTMP[5], bCST], [b_PTm], out=t_PTm[:].rearrange("p (a b) -> p a b", b=128),
               in0=t_PT[:].rearrange("p (a b) -> p a b", b=128), in1=V3(mask_f[:, 0:1], [[0, 4], [1, 128]]), op=ALU.mult)
            yield

        def main(d, g, hs):
            fwd = (d == 0)
            sidx = (l * 2 + d) * 4 + h
            gc0 = g * 512
            (t_b, t_M, t_ai, t_PTm), (b_b, b_M, b_ai, b_PTm) = HSETS[hs]
            SM, bSM = SMALLH[hs], bSMALLH[hs]
            PEC = SM[:, 44:48]
            for n in range(4):
                c0 = gc0 + n * 128
                OP("pe", "matmul", [bKCV[g], bQC[g]], [bPAT], PAT[:, n * 128:(n + 1) * 128], lhsT=KCV[:, c0:c0 + 128],
                   rhs=QC[:, c0:c0 + 128], start=True, stop=True)
            for n in range(4):
                c0 = gc0 + n * 128
                OP("pe", "transpose", [bKCV[g], bCB], [PKTB], PKT[:, n * 128:(n + 1) * 128], KCV[:, c0:c0 + 128], ident_bf)
            yield
            OP("dve", "tensor_tensor", [bPAT, b_PTm], [bTB[0]], out=TB[0][:], in0=PAT[:, :], in1=t_PTm[:], op=ALU.mult)
            ST = TB[0]
            OP("dve", "tensor_tensor", [bQC[g], b_ai], [bTB[1]], out=TB[1][:], in0=QC[:, gc0:gc0 + 512], in1=t_ai[:], op=ALU.mult)
            Qa = TB[1]
            for n in range(4):
                OP("act", "activation", [PKTB, bSM], [bTB[2]], out=TB[2][:, n * 128:(n + 1) * 128], in_=PKT[:, n * 128:(n + 1) * 128],
                   func=AF.Identity, scale=PEC[:, n:n + 1])
            KP = TB[2]
            yield
            for n in range(4):
                t = g * 4 + n
                OP("pe", "matmul", [bTB[2], bVTOK[g]], [bPKV], PKV[:, n * 128:(n + 1) * 128], lhsT=KP[:, n * 128:(n + 1) * 128],
                   rhs=VTOK[:, t, h * 128:(h + 1) * 128], start=True, stop=True)
                OP("pe", "matmul", [bTB[2], bONES], [bPMISC], PMISC[:, n * 128:(n + 1) * 128], lhsT=KP[:, n * 128:(n + 1) * 128],
                   rhs=ONES[:], start=True, stop=True)
            yield
            norder = list(range(4)) if fwd else list(range(3, -1, -1))
            for n in norder:
                t = g * 4 + n
                sq_, ts = t // job.tps, t % job.tps
                first = (ts == 0) if fwd else (ts == job.tps - 1)
                last = (ts == job.tps - 1) if fwd else (ts == 0)
                cs = n * 128
                if first:
                    if job.init_cache:
                        DMA("sp", "ld_sst", [], [bSST], SST[:], st_c[l, d, h, :, :])
                        OP("act", "copy", [bSST], [bSSTB], out=SSTB[:], in_=SST[:])
                        OP("dve", "tensor_copy", [bV], [bNREP], out=NREP[:], in_=V3(vcol("n0", sidx), [[0, 128]]))
                        OP("act", "copy", [bNREP], [bNREPB], out=NREPB[:], in_=NREP[:])
                    else:
                        OP("pool", "tensor_copy", [bCST], [bSST], out=SST[:], in_=ZR[:, 0:128])
                        OP("pool", "tensor_copy", [bCST], [bSSTB], out=SSTB[:], in_=ZR[:, 0:128])
                        OP("pool", "tensor_copy", [bCST], [bNREP], out=NREP[:], in_=ZR[:, 0:128])
                        OP("pool", "tensor_copy", [bCST], [bNREPB], out=NREPB[:], in_=ZR[:, 0:128])
                OP("pe", "matmul", [bTB[0], bVTOK[g]], [bPO], PO[:, cs:cs + 128], lhsT=VTOK[:, t, h * 128:(h + 1) * 128],
                   rhs=ST[:, cs:cs + 128], start=True, stop=False)
                OP("pe", "matmul", [bSSTB, bTB[1]], [bPO], PO[:, cs:cs + 128], lhsT=SSTB[:], rhs=Qa[:, cs:cs + 128], start=False, stop=True)
                OP("pe", "matmul", [bTB[0], bONES], [bPDEN], PDEN[:, cs:cs + 128], lhsT=ONES[:],
                   rhs=ST[:, cs:cs + 128], start=True, stop=False)
                OP("pe", "matmul", [bNREPB, bTB[1]], [bPDEN], PDEN[:, cs:cs + 128], lhsT=NREPB[:], rhs=Qa[:, cs:cs + 128], start=False, stop=True)
                lc = cs + 127 if fwd else cs
                OP("dve", "scalar_tensor_tensor", [bSST, b_ai, bPKV], [bSST], out=SST[:], in0=SST[:], scalar=t_ai[:, lc:lc + 1],
                   in1=PKV[:, cs:cs + 128], op0=ALU.mult, op1=ALU.add)
                OP("dve", "scalar_tensor_tensor", [bNREP, b_ai, bPMISC], [bNREP], out=NREP[:], in0=NREP[:], scalar=t_ai[:, lc:lc + 1],
                   in1=PMISC[:, cs:cs + 128], op0=ALU.mult, op1=ALU.add)
                if last and job.state_out:
                    DMA("sp", "st_sst", [bSST], [], ns_c[sq_, l, d, h, :, :], SST[:])
                    DMA("sp", "st_nrep", [bNREP], [], ns_n[sq_, l, d, h, :].rearrange("(p o) -> p o", o=1), NREP[:, 0:1])
                if not last:
                    OP("act", "copy", [bSST], [bSSTB], out=SSTB[:], in_=SST[:])
                    OP("act", "copy", [bNREP], [bNREPB], out=NREPB[:], in_=NREP[:])
                yield
            OP("act", "copy", [bPDEN], [bTMP[11]], out=TMP[11][:], in_=PDEN[:, :])
            yield
            OP("dve", "scalar_tensor_tensor", [bTMP[11]], [bTMP[9]], out=TMP[9][:], in0=TMP[11][:], scalar=-1.0, in1=TMP[11][:],
               op0=ALU.mult, op1=ALU.max)
            OP("dve", "tensor_tensor", [bTMP[9], bEMH[hs]], [bTMP[9]], out=TMP[9][:], in0=TMP[9][:], in1=EMH[hs][:], op=ALU.max)
            OP("act", "activation", [bTMP[9]], [bTMP[10]], out=TMP[10][:], in_=TMP[9][:], func=AF.Ln)
            OP("act", "activation", [bTMP[10]], [bTMP[10]], out=TMP[10][:], in_=TMP[10][:], func=AF.Exp, scale=-1.0)
            yield
            if fwd:
                OP("dve", "tensor_tensor", [bPO, bTMP[10]], [bHACC[g]], out=HACC[:, gc0:gc0 + 512], in0=PO[:, :], in1=TMP[10][:], op=ALU.mult)
            else:
                OP("dve", "tensor_tensor", [bPO, bTMP[10]], [bTMP[11]], out=TMP[11][:], in0=PO[:, :], in1=TMP[10][:], op=ALU.mult)
                OP("pool", "tensor_tensor", [bTMP[11], bHACC[g]], [bHACC[g]], out=HACC[:, gc0:gc0 + 512], in0=TMP[11][:],
                   in1=HACC[:, gc0:gc0 + 512], op=ALU.add)
            yield

        def run_units():
            drain(prep(units[0][0], units[0][1], 0, 0))
            for u, (d, g) in enumerate(units):
                nxt = prep(units[u + 1][0], units[u + 1][1], (u + 1) % 2, u + 1) if u + 1 < len(units) else None
                interleave(main(d, g, u % 2), nxt)

        def fin():
            for g in range(G):
                ps, bps = inproj(job, g, lambda c: wv[:, c, 2, :], sbuf_)
                for _ in head_finalize(job, l, g, ps, bps, AF.Sigmoid, HACC[:, g * 512:(g + 1) * 512], bHACC[g], vcol("mn", l * 4 + h), 4 + h):
                    yield

        return qpass, run_units, fin

    def ffn(job, l):
        j = job.cond
        PSS = [(PA, bPA), (PB_, bPB), (PAT, bPAT), (PKV, bPKV)]
        for tg in range(job.L // 1024):
            k = 0
            for b in range(6):
                ncol = 512 if b < 5 else 256
                sg, bsg = w_take()
                su, bsu = w_take()
                gv = sg[:, 0:8 * ncol].rearrange("p (c n) -> p c n", c=8)
                uv = su[:, 0:8 * ncol].rearrange("p (c n) -> p c n", c=8)
                for fcl in range(ncol // 128):
                    fc = b * 4 + fcl
                    for sub in range(2):
                        g = tg * 2 + sub
                        gc = slice(g * 512, (g + 1) * 512)
                        (p1, bp1), (p2, bp2) = PSS[(k % 2) * 2], PSS[(k % 2) * 2 + 1]
                        k += 1
                        for c in range(8):
                            OP("pe", "matmul", [bsg, bHT[g]], [bp1], p1[:, :], lhsT=gv[:, c, fcl * 128:(fcl + 1) * 128], rhs=HT[:, c, gc],
                               start=(c == 0), stop=(c == 7))
                        for c in range(8):
                            OP("pe", "matmul", [bsu, bHT[g]], [bp2], p2[:, :], lhsT=uv[:, c, fcl * 128:(fcl + 1) * 128], rhs=HT[:, c, gc],
                               start=(c == 0), stop=(c == 7))
                        tt, btt = TMP[k % 4], bTMP[k % 4]
                        OP("act", "activation", [bp1], [btt], out=tt[:], in_=p1[:, :], func=AF.Silu)
                        OP("dve", "tensor_tensor", [btt, bp2], [bABUF[sub]], out=ABUF[:, fc * 1024 + sub * 512:fc * 1024 + (sub + 1) * 512],
                           in0=tt[:], in1=p2[:, :], op=ALU.mult)
            for sub in range(2):
                g = tg * 2 + sub
                DMA("sp", "ld_xg%d" % sub, [], [bXG[sub]], XG[sub][:], xd[job.name][:, :, g * 512:(g + 1) * 512])
            for dc in range(8):
                sd, bsd = w_take()
                dv = sd[:, 0:NFF * 128].rearrange("p (f n) -> p f n", f=NFF)
                for sub in range(2):
                    p1, bp1 = PSS[(dc * 2 + sub) % 4]
                    for fc in range(NFF):
                        OP("pe", "matmul", [bsd, bABUF[sub]], [bp1], p1[:, :], lhsT=dv[:, fc, :],
                           rhs=ABUF[:, fc * 1024 + sub * 512:fc * 1024 + (sub + 1) * 512], start=(fc == 0), stop=(fc == NFF - 1))
                    OP("dve", "scalar_tensor_tensor", [bp1, bMODS, bXG[sub]], [bXG[sub]], out=XG[sub][:, dc, :], in0=p1[:, :],
                       scalar=modc(l, 5, dc, j), in1=XG[sub][:, dc, :], op0=ALU.mult, op1=ALU.add)
            for sub in range(2):
                g = tg * 2 + sub
                DMA("sp", "st_xg%d" % sub, [bXG[sub]], [bXD[job.name][g]], xd[job.name][:, :, g * 512:(g + 1) * 512], XG[sub][:])

    bXD = {"S": [Buf("xdS%d" % g) for g in range(4)], "P": [Buf("xdP%d" % g) for g in range(2)]}

    for job in jobs:
        fence_all()
        for t in range(job.NT):
            g, n = t // 4, t % 4
            xt, bxt = XTOK[t % 2], bXTOK[t % 2]
            xg, bxg = XG[g % 2], bXG[g % 2]
            DMA("sp", "ld_xtok%d" % (t % 2), [], [bxt], xt[:], xin[job.name][t * 128:(t + 1) * 128, :])
            for c in range(8):
                ps, bps = (PA, bPA) if c < 4 else (PB_, bPB)
                OP("pe", "transpose", [bxt, bCST], [bps], ps[:, (c % 4) * 128:(c % 4 + 1) * 128], xt[:, c * 128:(c + 1) * 128], ident)
            OP("act", "copy", [bPA], [bxg], out=xg[:, 0:4, n * 128:(n + 1) * 128], in_=PA[:, :].rearrange("p (c n) -> p c n", c=4))
            OP("dve", "tensor_copy", [bPB], [bxg], out=xg[:, 4:8, n * 128:(n + 1) * 128], in_=PB_[:, :].rearrange("p (c n) -> p c n", c=4))
            if n == 3:
                DMA("sp", "st_xg%d" % (g % 2), [bxg], [bXD[job.name][g]], xd[job.name][:, :, g * 512:(g + 1) * 512], xg[:])
        for l in range(nlayers):
            fence_all()
            for g in range(job.G):
                xg, bxg = XG[g % 2], bXG[g % 2]
                DMA("sp", "ld_xg%d" % (g % 2), [bXD[job.name][g]], [bxg], xg[:], xd[job.name][:, :, g * 512:(g + 1) * 512])
                norm_to_ht(job, l, 0, g, xg, bxg)
            fence_all()
            if do_hg:
                slot, sbuf_ = w_take()
                make_vtok(job, slot, sbuf_)
                prev_fin = None
                for h in range(4):
                    slot, sbuf_ = w_take()
                    qp_, p0_, ru, fn_ = hg_head(job, l, h, slot, sbuf_)
                    qp = qp_()
                    next(qp)
                    three_way(prev_fin, qp, p0_())
                    ru()
                    prev_fin = fn_()
                drain(prev_fin)
            fence_all()
            if do_ml:
                ml_gates(job, l)
                slot, sbuf_ = w_take()
                make_vtok(job, slot, sbuf_)
                prev_fin = None
                for h in range(4):
                    slot, sbuf_ = w_take()
                    qp, ru, fn_ = ml_head(job, l, h, slot, sbuf_)
                    drain(qp())
                    ru()
                    drain(fn_())
            fence_all()
            if do_hg or do_ml:
                s0, bs0 = w_take()
                s1, bs1 = w_take()
                wo = [s0[:, 0:4096].rearrange("p (c n) -> p c n", c=8), s1[:, 0:4096].rearrange("p (c n) -> p c n", c=8)]
                bwo = [bs0, bs1]
                e_list = [e for e in range(8) if (do_hg and e < 4) or (do_ml and e >= 4)]
                for g in range(job.G):
                    gc = slice(g * 512, (g + 1) * 512)
                    xg, bxg = XG[g % 2], bXG[g % 2]
                    DMA("sp", "ld_xg%d" % (g % 2), [bXD[job.name][g]], [bxg], xg[:], xd[job.name][:, :, gc])
                    DMA("sp", "ld_mixg", [bMIXD[job.name][g]], [bMIXG], MIXG[:], mixd[job.name][:, :, gc])
                    for dc in range(8):
                        ps, bps = (PA, bPA) if dc % 2 == 0 else (PB_, bPB)
                        for ei, e in enumerate(e_list):
                            OP("pe", "matmul", [bwo[dc // 4], bMIXG], [bps], ps[:, :], lhsT=wo[dc // 4][:, e, (dc % 4) * 128:(dc % 4 + 1) * 128],
                               rhs=MIXG[:, e, :], start=(ei == 0), stop=(ei == len(e_list) - 1))
                        OP("dve", "scalar_tensor_tensor", [bps, bMODS, bxg], [bxg], out=xg[:, dc, :], in0=ps[:, :],
                           scalar=modc(l, 2, dc, job.cond), in1=xg[:, dc, :], op0=ALU.mult, op1=ALU.add)
                    DMA("sp", "st_xg%d" % (g % 2), [bxg], [bXD[job.name][g]], xd[job.name][:, :, gc], xg[:])
                    if do_ffn:
                        norm_to_ht(job, l, 1, g, xg, bxg)
            elif do_ffn:
                for g in range(job.G):
                    xg, bxg = XG[g % 2], bXG[g % 2]
                    DMA("sp", "ld_xg%d" % (g % 2), [bXD[job.name][g]], [bxg], xg[:], xd[job.name][:, :, g * 512:(g + 1) * 512])
                    norm_to_ht(job, l, 1, g, xg, bxg)
            fence_all()
            if do_ffn:
                ffn(job, l)
        fence_all()
        fo = voff["fn"]
        for g in range(job.G):
            xg, bxg = XG[g % 2], bXG[g % 2]
            DMA("sp", "ld_xg%d" % (g % 2), [bXD[job.name][g]], [bxg], xg[:], xd[job.name][:, :, g * 512:(g + 1) * 512])
            OP("act", "activation", [bxg], [bSQ], out=SQ[:].rearrange("p c n -> p (c n)"),
               in_=xg[:].rearrange("p c n -> p (c n)"), func=AF.Square)
            rms_rstd([SQ[:, c, :] for c in range(8)], [bSQ], 1.0 / D, PMISC, bPMISC)
            for c in range(8):
                OP("dve", "scalar_tensor_tensor", [bxg, bV, bRSTD], [bxg], out=xg[:, c, :], in0=xg[:, c, :],
                   scalar=V[:, fo + c:fo + c + 1], in1=RSTD[:], op0=ALU.mult, op1=ALU.mult)
            for n in range(4):
                t = g * 4 + n
                yt, byt = YTOK[t % 2], bYTOK[t % 2]
                for c in range(8):
                    ps, bps = (PA, bPA) if c < 4 else (PB_, bPB)
                    OP("pe", "transpose", [bxg, bCST], [bps], ps[:, (c % 4) * 128:(c % 4 + 1) * 128], xg[:, c, n * 128:(n + 1) * 128], ident)
                OP("act", "copy", [bPA], [byt], out=yt[:, 0:512], in_=PA[:, :])
                OP("dve", "tensor_copy", [bPB], [byt], out=yt[:, 512:1024], in_=PB_[:, :])
                DMA("sp", "st_ytok%d" % (t % 2), [byt], [], yout[job.name][t * 128:(t + 1) * 128, :], yt[:])

    allb = list(bufs.values()) + PSB + [PKTB] + bHT + get_alias() + bXD["S"] + bXD["P"] + bMIXD["S"] + bMIXD["P"]
    S.finish("sp", allb)
    S.emit(st)
    st.close()
    return nc, S


_CACHE = {}


def _f32(a):
    return np.ascontiguousarray(np.asarray(a), dtype=np.float32)


def kernel(x_prompt, x_sample, state_hgrn, state_mlstm_c, state_mlstm_n, state_mlstm_m, c, c_ctx,
           norm1_w, norm2_w, w_mod, b_mod, w_in, conv_w, conv_b, ml_gate_b, hg_lb_logits,
           hg_norm_w, ml_norm_w, w_out, w_gate, w_up, w_down, final_norm_w, _opts=None):
    opts = _opts or {}
    key = tuple(sorted(opts.items()))
    if key not in _CACHE:
        _CACHE[key] = build_program(**opts)
    nc, _ = _CACHE[key]
    cst, sel = host_consts()
    shared = {
        "norm1_w": _f32(norm1_w), "norm2_w": _f32(norm2_w), "w_mod": _f32(w_mod), "b_mod": _f32(b_mod),
        "w_in": _f32(w_in), "conv_w": _f32(conv_w), "conv_b": _f32(conv_b), "ml_gate_b": _f32(ml_gate_b),
        "hg_lb_logits": _f32(hg_lb_logits), "hg_norm_w": _f32(hg_norm_w), "ml_norm_w": _f32(ml_norm_w),
        "w_out": _f32(w_out), "w_gate": _f32(w_gate), "w_up": _f32(w_up), "w_down": _f32(w_down),
        "final_norm_w": _f32(final_norm_w), "consts": cst, "selc": sel,
    }
    x_prompt = _f32(x_prompt)
    x_sample = _f32(x_sample)
    c = _f32(c)
    c_ctx = _f32(c_ctx)
    in_maps = []
    for i in range(8):
        m = dict(shared)
        m["x_s"] = x_sample[i]
        m["x_p"] = x_prompt[4 * i:4 * i + 4].reshape(1024, D)
        m["cvec"] = np.concatenate([c[i].reshape(8, 128), c_ctx.reshape(8, 128)], axis=0)
        m["st_hg"] = _f32(state_hgrn[i])
        m["st_c"] = _f32(state_mlstm_c[i])
        m["st_n"] = _f32(state_mlstm_n[i]).reshape(32, 128)
        m["st_m"] = _f32(state_mlstm_m[i]).reshape(1, 32)
        in_maps.append(m)
    res = run_bass_kernel_spmd(nc, in_maps, core_ids=list(range(8)))
    rs = res.results
    y_prompt = np.concatenate([r["y_p"].reshape(4, 256, D) for r in rs], axis=0)
    y_sample = np.stack([r["y_s"] for r in rs], axis=0)
    ns_hg = np.concatenate([r["ns_hg"] for r in rs], axis=0)
    ns_c = np.concatenate([r["ns_c"] for r in rs], axis=0)
    ns_n = np.concatenate([r["ns_n"] for r in rs], axis=0)
    ns_m = np.concatenate([r["ns_m"] for r in rs], axis=0)
    return (y_prompt.astype(np.float32), y_sample.astype(np.float32), ns_hg.astype(np.float32),
            ns_c.astype(np.float32), ns_n.astype(np.float32), ns_m.astype(np.float32))
```
